# Optimizing a Trainium2 kernel written in Bass

```python
import jax, jax.numpy as jnp
from jax import lax
import numpy as np

D_MODEL = 2048
BATCH = 4
SEQ = 2048
DEPTH = 1

D_MIX = D_MODEL
W_POOL = D_MIX // 2
W_LRU = D_MIX - W_POOL
POOL_WINDOWS = (2, 4, 8, 16)
N_POOL_GROUPS = len(POOL_WINDOWS)
POOL_GROUP = W_POOL // N_POOL_GROUPS
N_LRU_HEADS = 4
LRU_HEAD = W_LRU // N_LRU_HEADS
N_DIR = 2
CONV_WIDTH = 4
LRU_C = 8.0
EPS = 1e-6

kernel_name = "bidir_hybrid_pool_rglru_block"


def rms_norm(x, g):
    xf = x.astype(jnp.float32)
    y = xf * lax.rsqrt(jnp.mean(xf * xf, axis=-1, keepdims=True) + EPS)
    return (y * g.astype(jnp.float32)).astype(x.dtype)


def pool_mixer(u, w_pool, b_pool, pool_scale):
    B, S, _ = u.shape
    uf = u.astype(jnp.float32)
    cs = jnp.concatenate([jnp.zeros((B, 1, W_POOL), jnp.float32), jnp.cumsum(uf, axis=1)], axis=1)
    t = jnp.arange(S)
    outs = []
    for g, w in enumerate(POOL_WINDOWS):
        lo = w // 2
        hi = w - lo - 1
        start = jnp.maximum(t - lo, 0)
        end = jnp.minimum(t + hi, S - 1) + 1
        csg = cs[..., g * POOL_GROUP:(g + 1) * POOL_GROUP]
        total = csg[:, end, :] - csg[:, start, :]
        cnt = (end - start).astype(jnp.float32)[None, :, None]
        outs.append(total / cnt - uf[..., g * POOL_GROUP:(g + 1) * POOL_GROUP])
    pooled = jnp.stack(outs, axis=2).astype(u.dtype)
    mixed = jnp.einsum('bsgp,gpq->bsgq', pooled, w_pool) + b_pool
    return mixed.reshape(B, S, W_POOL) * pool_scale


def centred_dwconv(u, conv_w, conv_b):
    S = u.shape[1]
    left = CONV_WIDTH // 2
    up = jnp.pad(u, ((0, 0), (left, CONV_WIDTH - left - 1), (0, 0)))
    out = conv_b
    for k in range(CONV_WIDTH):
        out = out + up[:, k:k + S, :] * conv_w[k]
    return out


def linear_scan(a, b, reverse):
    def combine(l, r):
        a_l, b_l = l
        a_r, b_r = r
        return a_l * a_r, a_r * b_l + b_r
    _, h = lax.associative_scan(combine, (a, b), axis=1, reverse=reverse)
    return h


def bidir_rg_lru(u, w_gate, b_gate, lru_lambda):
    B, S, _ = u.shape
    uh = u.reshape(B, S, N_LRU_HEADS, LRU_HEAD)
    gates = jnp.einsum('bshc,nhce->nbshe', uh, w_gate) + b_gate[:, None, None]
    gates = jax.nn.sigmoid(gates.astype(jnp.float32))
    r = gates[..., :LRU_HEAD].reshape(N_DIR, B, S, W_LRU)
    i = gates[..., LRU_HEAD:].reshape(N_DIR, B, S, W_LRU)
    log_a = -LRU_C * r * jax.nn.softplus(-lru_lambda.astype(jnp.float32))[:, None, None, :]
    a = jnp.exp(log_a)
    mult = jnp.sqrt(-jnp.expm1(2.0 * log_a))
    bx = mult * i * u.astype(jnp.float32)[None]
    h_f = linear_scan(a[0], bx[0], reverse=False)
    h_b = linear_scan(a[1], bx[1], reverse=True)
    return (h_f + h_b).astype(u.dtype)


def setup_inputs(seed: int = 0) -> dict:
    key = jax.random.key(seed)
    ks = jax.random.split(key, 24)
    f32 = jnp.float32
    nrm = lambda k, shape, s: jax.random.normal(k, shape, f32) * s
    a0 = jax.random.uniform(ks[14], (DEPTH, N_DIR, W_LRU), f32, 0.9, 0.999)
    p = a0 ** (1.0 / LRU_C)
    lru_lambda = jnp.log(p) - jnp.log1p(-p)
    return {
        "x": nrm(ks[0], (BATCH, SEQ, D_MODEL), 1.0),
        "c": nrm(ks[1], (BATCH, D_MODEL), 1.0),
        "norm_g": 1.0 + nrm(ks[2], (DEPTH, D_MODEL), 0.02),
        "w_ada": nrm(ks[3], (DEPTH, D_MODEL, 3 * D_MODEL), 0.5 * D_MODEL ** -0.5),
        "b_ada": nrm(ks[4], (DEPTH, 3 * D_MODEL), 0.02),
        "w_in": nrm(ks[5], (DEPTH, D_MODEL, 2 * D_MIX), D_MODEL ** -0.5),
        "b_in": nrm(ks[6], (DEPTH, 2 * D_MIX), 0.02),
        "w_pool": nrm(ks[7], (DEPTH, N_POOL_GROUPS, POOL_GROUP, POOL_GROUP), POOL_GROUP ** -0.5),
        "b_pool": nrm(ks[8], (DEPTH, N_POOL_GROUPS, POOL_GROUP), 0.02),
        "pool_scale": 1.0 + nrm(ks[9], (DEPTH, W_POOL), 0.1),
        "conv_w": nrm(ks[10], (DEPTH, CONV_WIDTH, W_LRU), CONV_WIDTH ** -0.5),
        "conv_b": nrm(ks[11], (DEPTH, W_LRU), 0.02),
        "w_gate": nrm(ks[12], (DEPTH, N_DIR, N_LRU_HEADS, LRU_HEAD, 2 * LRU_HEAD), LRU_HEAD ** -0.5),
        "b_gate": nrm(ks[13], (DEPTH, N_DIR, N_LRU_HEADS, 2 * LRU_HEAD), 0.02),
        "lru_lambda": lru_lambda,
        "out_norm_pool_g": 1.0 + nrm(ks[15], (DEPTH, W_POOL), 0.02),
        "out_norm_lru_g": 1.0 + nrm(ks[16], (DEPTH, W_LRU), 0.02),
        "w_out": nrm(ks[17], (DEPTH, D_MIX, D_MODEL), D_MIX ** -0.5),
        "b_out": nrm(ks[18], (DEPTH, D_MODEL), 0.02),
        "final_norm_g": 1.0 + nrm(ks[19], (D_MODEL,), 0.02),
    }


def reference(x, c, norm_g, w_ada, b_ada, w_in, b_in, w_pool, b_pool, pool_scale,
              conv_w, conv_b, w_gate, b_gate, lru_lambda, out_norm_pool_g, out_norm_lru_g,
              w_out, b_out, final_norm_g):
    c_act = jax.nn.silu(c)
    for l in range(DEPTH):
        mod = c_act @ w_ada[l] + b_ada[l]
        shift, scale, gate = jnp.split(mod, 3, axis=-1)
        h = rms_norm(x, norm_g[l]) * (1.0 + scale[:, None, :]) + shift[:, None, :]
        z = h @ w_in[l] + b_in[l]
        u_pool = z[..., :W_POOL]
        u_lru = z[..., W_POOL:D_MIX]
        g_pool = z[..., D_MIX:D_MIX + W_POOL]
        g_lru = z[..., D_MIX + W_POOL:]
        y_pool = pool_mixer(u_pool, w_pool[l], b_pool[l], pool_scale[l])
        y_lru = bidir_rg_lru(centred_dwconv(u_lru, conv_w[l], conv_b[l]),
                             w_gate[l], b_gate[l], lru_lambda[l])
        y_pool = rms_norm(y_pool, out_norm_pool_g[l]) * jax.nn.silu(g_pool)
        y_lru = rms_norm(y_lru, out_norm_lru_g[l]) * jax.nn.silu(g_lru)
        y = jnp.concatenate([y_pool, y_lru], axis=-1) @ w_out[l] + b_out[l]
        x = x + gate[:, None, :] * y
    return rms_norm(x, final_norm_g)
```

```python
import math
import numpy as np
import concourse.bass as bass
import concourse.mybir as mybir
from concourse.bass_utils import run_bass_kernel_spmd

F32 = mybir.dt.float32
BF16 = mybir.dt.bfloat16
U8 = mybir.dt.uint8
AF = mybir.ActivationFunctionType
ALU = mybir.AluOpType

NCORES = 8
D = 2048
S = 2048
T = 1024
HALO = 8
TW = T + HALO
KC = 16
EPS = 1e-6
WINS = (2, 4, 8, 16)

_c = 0
def _col(n):
    global _c
    r = _c
    _c += n
    return r
C_NG = _col(16); C_BIN = _col(32); C_CW = _col(40); C_CB = _col(8); C_BG = _col(64); C_LAM = _col(16)
C_BP = _col(8); C_PS = _col(8); C_GNP = _col(8); C_GNL = _col(8); C_OH = _col(4); C_AW = _col(4); C_BW = _col(4)
C_AL = _col(1); C_BE = _col(1); C_OHP = _col(8); C_BADA = _col(24); C_INVC = _col(32)
NP = _c
C_BGH = _col(64); C_C1H = _col(16); C_C1 = _col(16); C_C1Q = _col(16); C_S1 = _col(8); C_S2 = _col(8); C_S3 = _col(8)
C_TMP = _col(16 * 6); C_HALF = _col(1); C_MHALF = _col(1)
NPD = _c

PW = 256
PIECE_COLS = [i * PW for i in range(16)]


class Buf:
    __slots__ = ("w", "r")
    def __init__(self):
        self.w = []
        self.r = []


class Sched:
    def __init__(self, nc, eng_sems):
        self.nc = nc
        self.ops = {e: [] for e in eng_sems}
        self.sem = eng_sems
        self.cnt = {e: 0 for e in eng_sems}
        self.waited = {e: {} for e in eng_sems}
        self.nops = 0
        self.limit = None
        self.marks = []
        self.dma_sems = []

    def _waits(self, eng, toks):
        best = {}
        for (s, v) in toks:
            k = id(s)
            if k not in best or best[k][1] < v:
                best[k] = (s, v)
        out = []
        for k, (s, v) in best.items():
            if self.waited[eng].get(k, 0) >= v:
                continue
            self.waited[eng][k] = v
            out.append((s, v))
        return out

    def op(self, eng, fn, reads=(), writes=(), extra=(), dma_sem=None, sig_eng=None):
        self.nops += 1
        if self.limit is not None and self.nops > self.limit:
            return None
        toks = list(extra)
        for b in reads:
            toks += b.w
        for b in writes:
            toks += b.w + b.r
        waits = self._waits(eng, toks)
        if dma_sem is not None:
            dma_sem[1] += 16
            tok = (dma_sem[0], dma_sem[1])
            self.ops[eng].append((fn, waits, ("dma", dma_sem[0])))
        else:
            se = sig_eng or eng
            self.cnt[se] += 1
            tok = (self.sem[se], self.cnt[se])
            self.ops[eng].append((fn, waits, ("inc", self.sem[se])))
        for b in reads:
            b.r.append(tok)
        for b in writes:
            b.w = [tok]
            b.r = []
        return tok

    def emit(self, eng, e):
        for fn, waits, sig in self.ops[eng]:
            for (s, v) in waits:
                e.wait_ge(s, v)
            ins = fn(e)
            if sig[0] == "dma":
                ins.then_inc(sig[1], 16)
            else:
                ins.then_inc(sig[1], 1)


def I(method, **kw):
    return lambda e: getattr(e, method)(**kw)


def G(items):
    def fn(e):
        ins = None
        for (m, kw) in items:
            ins = getattr(e, m)(**kw)
        return ins
    return fn


def build_program(debug=(), limit=None, fake_cc=False, shrink_test=False):
    nc = bass.Bass("TRN2", target_bir_lowering=False)
    dt_in = lambda name, shape: nc.dram_tensor(name, shape, F32, kind="ExternalInput").ap()
    x_d = dt_in("x_loc", [TW, D])
    c_d = dt_in("c_own", [128, 16])
    wa_d = dt_in("w_ada", [D, 3072])
    pp_d = dt_in("pp", [128, NP])
    win_d = dt_in("w_in", [D, 4096])
    wg_d = dt_in("w_gate", [2, 4, 256, 512])
    wp_d = dt_in("w_pool", [4, 256, 256])
    wo_d = dt_in("w_out", [D, D])
    id_d = dt_in("ident", [128, 128])
    fg_d = dt_in("fg", [D])
    bo_d = dt_in("bo", [D])
    out_d = nc.dram_tensor("out_loc", [T, D], F32, kind="ExternalOutput").ap()
    wo_bf = nc.dram_tensor("wo_bf", [D, D], BF16).ap()
    wg_bf = nc.dram_tensor("wg_bf", [2, 4, 256, 512], BF16).ap()
    cin1 = [nc.dram_tensor("cin1a", [128, 16], F32).ap(), nc.dram_tensor("cin1b", [128, 8], F32).ap()]
    cout1 = [nc.dram_tensor("cout1a", [2 * 128, 16], F32).ap(), nc.dram_tensor("cout1b", [2 * 128, 8], F32).ap()]
    cin2 = [nc.dram_tensor("cin2%s" % t, [128, 8], F32).ap() for t in "ab"]
    cout2 = [nc.dram_tensor("cout2%s" % t, [2 * 128, 8], F32).ap() for t in "ab"]
    dbg_d = {}
    for name, shape, dtp in debug:
        dbg_d[name] = nc.dram_tensor("dbg_" + name, list(shape), dtp, kind="ExternalOutput").ap()

    R_UL, R_UP, R_H, R_SG, R_PM, R_SQ, R_RS, R_WP, R_C = 0, 37440, 70976, 120384, 153152, 169728, 173824, 177920, 182016
    TOTAL = 189000 + 3 * 4096 + 8192
    R_T3 = 189000
    R_RING3 = 189000 + 3 * 4096
    if shrink_test:
        R_SG = R_UP
        R_PM, R_SQ, R_RS, R_WP, R_C = [r - 32768 for r in (R_PM, R_SQ, R_RS, R_WP, R_C)]
        TOTAL -= 32768

    from contextlib import ExitStack
    with ExitStack() as es:
        big = es.enter_context(nc.sbuf_tensor("big", [128, TOTAL], U8))
        ps = es.enter_context(nc.psum_tensor("ps", [128, 4096], F32))
        def newsem(name):
            return es.enter_context(nc.semaphore(name))
        eng_sems = {e: newsem("s_" + e) for e in ("pe", "act", "dve", "pool", "sp", "cc")}
        S_ = Sched(nc, eng_sems)
        S_.limit = limit
        def dsem(name):
            d = [newsem("d_" + name), 0]
            S_.dma_sems.append(d)
            return d

        def v(off, nbytes, dtp):
            return big[:, off:off + nbytes].bitcast(dtp)
        cpos = [R_C]
        def calloc(nelem, dtp=F32, esz=4):
            off = cpos[0]
            cpos[0] += ((nelem * esz + 63) // 64) * 64
            assert cpos[0] <= TOTAL, cpos[0]
            return v(off, nelem * esz, dtp)

        pp = calloc(NPD)
        ident_f = calloc(128)
        ident_b = calloc(128, BF16, 2)
        ones_f = calloc(128)
        cT = calloc(16)
        cact = calloc(16)
        mod_loc = calloc(24)
        mod_all = calloc(48)
        sel = calloc(48)
        cactb = calloc(16, BF16, 2)
        Gs = calloc(16)
        ss = calloc(16); sq_s = calloc(16); rstd = calloc(16)
        ss2 = calloc(8); sq2 = calloc(8); rstd2 = calloc(8)
        cko = calloc(16); ckall = calloc(32); carry = calloc(8)
        diag = calloc(128)
        def P(c, n=1):
            return pp[:, c:c + n]

        ULuc = v(R_UL, 37440, F32).rearrange("p (c t) -> p c t", c=9)
        UP = v(R_UP, 33536, F32).rearrange("p (c t) -> p c t", c=8)
        xring = [v(R_UL + s * 8192, 8192, F32) for s in range(3)]
        xn = [v(R_UL + 24576 + g * 16384, 16384, BF16).rearrange("p (i d) -> p i d", i=4) for g in range(2)]
        junk = v(R_UL + 57344, 4096, BF16)
        hT = v(R_H, 33024, BF16).rearrange("p (k t) -> p k t", k=16)
        ring = [v(R_H + 33024 + s * 8192, 8192, BF16).rearrange("p (k n) -> p k n", k=16) for s in range(2)] + \
               [v(R_RING3, 8192, BF16).rearrange("p (k n) -> p k n", k=16)]
        wa = v(R_H, 24576, BF16).rearrange("p (k n) -> p k n", k=16)
        HF = v(R_H, 32768, F32).rearrange("p (c t) -> p c t", c=8)
        Tb = [v(R_UP + i * 4096, 4096, F32) for i in range(6)] + [v(R_T3 + i * 4096, 4096, F32) for i in range(3)] + \
             [v(R_H + 32768 + i * 4096, 4096, F32) for i in range(3)]
        ucb = v(R_PM, 16384, BF16).rearrange("p (c t) -> p c t", c=8)
        woutB = v(R_UP, 32768, BF16).rearrange("p (k n) -> p k n", k=8)
        woutA = v(R_UL, 32768, BF16).rearrange("p (k n) -> p k n", k=8)
        SG = v(R_SG, 32768, BF16).rearrange("p (c t) -> p c t", c=16)
        tmpA = v(R_PM, 4192, F32); tmpB = v(R_PM + 4192, 4192, F32)
        ttmp = v(R_PM + 8384, 4096, F32)
        pooled = v(R_PM + 12480, 4096, BF16).rearrange("p (k t) -> p k t", k=2)
        wgr = [v(R_UP + 24576 + s * 2048, 2048, BF16).rearrange("p (k n) -> p k n", k=2) for s in range(3)]
        gate_bc = v(R_RING3, 8192, F32); fg_bc = v(R_PM + 8192, 8192, F32)
        SQ = v(R_SQ, 4096, F32); RS = v(R_RS, 4096, F32)
        gb_bc = v(R_SQ, 8192, F32)
        WPl = v(R_WP, 4096, BF16).rearrange("p (g k n) -> p g k n", g=4, k=2)
        xt = [v(R_H + 32768 + s * 8192, 8192, F32) for s in range(2)]
        ot = [v(R_H + s * 8192, 8192, F32) for s in range(2)]

        def bank(b, n=1):
            return ps[:, b * 512:(b + n) * 512]

        B = {}
        def bf(name):
            if name not in B:
                B[name] = Buf()
            return B[name]
        def bfs(fmt, rng):
            return [bf(fmt % i) for i in rng]
        def alias(new, olds):
            nb = bf(new)
            for o in olds:
                ob = bf(o)
                nb.r = nb.r + ob.w + ob.r

        def pe(fn, **k): return S_.op("pe", fn, **k)
        def act(fn, **k): return S_.op("act", fn, **k)
        def dve(fn, **k): return S_.op("dve", fn, **k)
        def pool(fn, **k): return S_.op("pool", fn, **k)
        def sp_dma(out, in_, sem, **k):
            return S_.op("sp", I("dma_start", out=out, in_=in_), dma_sem=sem, **k)
        def pool_dma(out, in_, sem, **k):
            return S_.op("pool", I("dma_start", out=out, in_=in_), dma_sem=sem, **k)
        def pool_cc(fn, **k):
            return S_.op("pool", fn, sig_eng="cc", **k)
        dbg_sem = dsem("dbg")
        def dump(name, src, reads):
            if name in dbg_d:
                sp_dma(dbg_d[name], src, dbg_sem, reads=reads)

        sp_dma(pp[:, 0:NP], pp_d, dsem("pp"), writes=[bf("pp")])
        sp_dma(cT, c_d, dsem("cT"), writes=[bf("cT")])
        sp_dma(ident_f, id_d, dsem("id"), writes=[bf("ident_f")])
        dve(I("memset", ap=ones_f, constant=1.0), writes=[bf("ones_f")])
        dve(I("tensor_copy", out=ident_b, in_=ident_f), reads=[bf("ident_f")], writes=[bf("ident_b")])
        dve(I("memset", ap=pp[:, C_HALF:C_HALF + 1], constant=1.0), writes=[bf("pp")])

        def tmpc(i):
            return pp[:, C_TMP + 16 * i:C_TMP + 16 * (i + 1)]
        lam = P(C_LAM, 16)
        W = [bf("pp")]
        dve(I("tensor_scalar", out=P(C_BGH, 64), in0=P(C_BG, 64), scalar1=0.5, scalar2=None, op0=ALU.mult), writes=W)
        dve(I("tensor_scalar", out=tmpc(0), in0=lam, scalar1=-1.0, scalar2=None, op0=ALU.mult), writes=W)
        dve(I("tensor_tensor", out=tmpc(0), in0=tmpc(0), in1=lam, op=ALU.max), writes=W)
        act(I("activation", out=tmpc(1), in_=tmpc(0), func=AF.Exp, scale=-1.0), writes=W)
        dve(I("tensor_scalar", out=tmpc(2), in0=tmpc(1), scalar1=2.0, scalar2=None, op0=ALU.add), writes=W)
        dve(I("reciprocal", out=tmpc(2), in_=tmpc(2)), writes=W)
        dve(I("tensor_tensor", out=tmpc(2), in0=tmpc(2), in1=tmpc(1), op=ALU.mult), writes=W)
        dve(I("tensor_tensor", out=tmpc(3), in0=tmpc(2), in1=tmpc(2), op=ALU.mult), writes=W)
        dve(I("tensor_scalar", out=tmpc(4), in0=tmpc(3), scalar1=1.0 / 13.0, scalar2=1.0 / 11.0, op0=ALU.mult, op1=ALU.add), writes=W)
        for cst in (1.0 / 9.0, 1.0 / 7.0, 1.0 / 5.0, 1.0 / 3.0, 1.0):
            dve(I("tensor_tensor", out=tmpc(4), in0=tmpc(4), in1=tmpc(3), op=ALU.mult), writes=W)
            dve(I("tensor_scalar", out=tmpc(4), in0=tmpc(4), scalar1=cst, scalar2=None, op0=ALU.add), writes=W)
        dve(I("tensor_tensor", out=tmpc(4), in0=tmpc(4), in1=tmpc(2), op=ALU.mult), writes=W)
        dve(I("tensor_scalar", out=tmpc(5), in0=lam, scalar1=-1.0, scalar2=0.0, op0=ALU.mult, op1=ALU.max), writes=W)
        dve(I("scalar_tensor_tensor", out=tmpc(5), in0=tmpc(4), scalar=2.0, in1=tmpc(5), op0=ALU.mult, op1=ALU.add), writes=W)
        dve(I("tensor_scalar", out=P(C_C1, 16), in0=tmpc(5), scalar1=-8.0, scalar2=None, op0=ALU.mult), writes=W)
        dve(I("tensor_scalar", out=P(C_C1H, 16), in0=tmpc(5), scalar1=-4.0, scalar2=None, op0=ALU.mult), writes=W)
        dve(I("tensor_scalar", out=P(C_C1Q, 16), in0=tmpc(5), scalar1=-8.0, scalar2=math.log(0.25), op0=ALU.mult, op1=ALU.add), writes=W)
        dve(I("tensor_tensor", out=P(C_S1, 8), in0=P(C_PS, 8), in1=P(C_GNP, 8), op=ALU.mult), writes=W)
        dve(I("tensor_tensor", out=P(C_S2, 8), in0=P(C_BP, 8), in1=P(C_S1, 8), op=ALU.mult), writes=W)
        dve(I("tensor_tensor", out=P(C_S3, 8), in0=P(C_BP, 8), in1=P(C_PS, 8), op=ALU.mult), writes=W)

        act(I("activation", out=cact, in_=cT, func=AF.Silu), reads=[bf("cT")], writes=[bf("cact")])
        dve(I("tensor_copy", out=cactb, in_=cact), reads=[bf("cact")], writes=[bf("cactb")])
        shift = sel[:, 0:16]; gate_fm = sel[:, 32:48]
        ada_ps = bank(7)[:, 0:48]

        win_v = win_d.rearrange("(k p) n -> p k n", p=128)
        wa_v = wa_d.rearrange("(p k) n -> p k n", k=16)
        ring_sem = [dsem("ring0"), dsem("ring1"), dsem("ring2")]
        pcount = [2]
        piece_tok = {}
        stage = [v(R_SG + s * 16384, 16384, F32).rearrange("p (k n) -> p k n", k=16) for s in range(2)]
        stage_sem = [dsem("stage0"), dsem("stage1")]
        def load_piece(kind, q):
            if kind == "ada" and q < 8:
                st = q % 2
                sp_dma(stage[st], wa_v[:, :, q * PW:(q + 1) * PW], stage_sem[st], writes=[bf("stage%d" % st)])
                s = q % 3
                slot, name = ring[s], "ring%d" % s
                tok = dve(I("tensor_copy", out=slot, in_=stage[st]), reads=[bf("stage%d" % st)], writes=[bf(name)])
                piece_tok[(kind, q)] = tok
                return (slot, name)
            s = pcount[0] % 3; pcount[0] += 1
            slot, name, sem = ring[s], "ring%d" % s, ring_sem[s]
            srcv = wa_v[:, :, q * PW:(q + 1) * PW] if kind == "ada" else win_v[:, :, PIECE_COLS[q]:PIECE_COLS[q] + PW]
            tok = pool_dma(slot, srcv, sem, writes=[bf(name)])
            piece_tok[(kind, q)] = tok
            return (slot, name)
        def ada_piece(q, sl):
            slot, sname = sl
            items = []
            for mc in range(2):
                j = q * 2 + mc
                for k in range(KC):
                    items.append(("matmul", dict(out=ada_ps[:, j:j + 1], lhsT=slot[:, k, mc * 128:(mc + 1) * 128], rhs=cactb[:, k:k + 1],
                                                 start=(k == 0), stop=(k == KC - 1))))
            pe(G(items), reads=[bf(sname), bf("cactb")], writes=[bf("bank7")])
        SEQ = [("ada", q) for q in range(8)] + [("win", q) for q in range(8)]
        for q in range(4):
            SEQ += [("ada", 8 + q), ("win", 8 + 2 * q), ("win", 9 + 2 * q)]
        slot_of = {}
        nxt = [0]
        def prefetch(upto):
            while nxt[0] < len(SEQ) and nxt[0] <= upto:
                kind, q = SEQ[nxt[0]]
                slot_of[(kind, q)] = load_piece(kind, q)
                nxt[0] += 1
        prefetch(1)
        pool_dma(WPl, wp_d.rearrange("g (k p) n -> p g k n", p=128), dsem("wp"), writes=[bf("WPl")])
        x_sem = [dsem("x0"), dsem("x1"), dsem("x2")]
        xdep = []
        NT = 9
        def rows(i):
            return (i * 128, 128) if i < 8 else (T, HALO)
        def x_load(i):
            s = i % 3
            r0, n = rows(i)
            S_.op("act", I("dma_start", out=xring[s][0:n, :], in_=x_d[r0:r0 + n, :]), dma_sem=x_sem[s], writes=[bf("xr%d" % s)])
        def xn_tile(i):
            return xn[i // 4][:, i % 4, :] if i < 8 else xn[0][0:HALO, 0, :]
        def f_square(i):
            s = i % 3; r0, n = rows(i)
            act(I("activation", out=junk[0:n, :], in_=xring[s][0:n, :], func=AF.Square, accum_out=ss[0:n, i:i + 1]),
                reads=[bf("xr%d" % s)], writes=[bf("junk"), bf("ss%d" % i)])
            act(I("activation", out=sq_s[0:n, i:i + 1], in_=ss[0:n, i:i + 1], func=AF.Ln, scale=1.0 / D, bias=EPS),
                reads=[bf("ss%d" % i)], writes=[bf("sq%d" % i)])
            act(I("activation", out=rstd[0:n, i:i + 1], in_=sq_s[0:n, i:i + 1], func=AF.Exp, scale=-0.5),
                reads=[bf("sq%d" % i)], writes=[bf("rstd%d" % i)])
        def f_scale(i):
            s = i % 3; r0, n = rows(i)
            nm = "xn%d" % i if i < 8 else "xnh"
            act(I("activation", out=xn_tile(i), in_=xring[s][0:n, :], func=AF.Copy, scale=rstd[0:n, i:i + 1]),
                reads=[bf("xr%d" % s), bf("rstd%d" % i)], writes=[bf(nm)])
        tp = [bank(5).bitcast(BF16)[:, 0:512], bank(6).bitcast(BF16)[:, 0:512]]
        tph = bank(4).bitcast(BF16)[:, 0:128].rearrange("p (j t) -> p j t", j=16)
        tpc = [0]
        def f_transposes(g):
            for j in range(KC):
                b_ = tpc[0] % 2; tpc[0] += 1
                items = [("transpose", dict(out=tp[b_][:, ii * 128:(ii + 1) * 128], in_=xn[g][:, ii, j * 128:(j + 1) * 128], identity=ident_b))
                         for ii in range(4)]
                pe(G(items), reads=bfs("xn%d", range(4 * g, 4 * g + 4)) + [bf("ident_b")], writes=[bf("tp%d" % b_)])
                dve(I("tensor_scalar", out=hT[:, j, g * 512:(g + 1) * 512], in0=tp[b_], scalar1=Gs[:, j:j + 1], scalar2=shift[:, j:j + 1],
                      op0=ALU.mult, op1=ALU.add),
                    reads=[bf("tp%d" % b_), bf("Gs"), bf("sel")], writes=[bf("hT")])
        for i in range(8):
            ada_piece(i, slot_of[("ada", i)])
            if i + 2 < 8:
                prefetch(i + 2)
        for i in range(3):
            x_load(i)
        for i in range(NT):
            f_square(i)
            if i >= 1:
                f_scale(i - 1)
                if i + 2 < NT:
                    x_load(i + 2)
        def ada_gather(t, lo, n):
            nm = "ab"[t]
            dve(I("tensor_tensor", out=mod_loc[:, lo:lo + n], in0=ada_ps[:, lo:lo + n], in1=P(C_BADA + lo, n), op=ALU.add),
                reads=[bf("bank7"), bf("pp")], writes=[bf("mod_loc" + nm)])
            bdma = pool_dma if t == 0 else sp_dma
            bdma(cin1[t], mod_loc[:, lo:lo + n], dsem("cin1" + nm), reads=[bf("mod_loc" + nm)], writes=[bf("cin1" + nm)])
            if fake_cc:
                pool_dma(cout1[t][0:128, :], cin1[t], dsem("fcc1a" + nm), reads=[bf("cin1" + nm)], writes=[bf("cout1" + nm)])
                pool_dma(cout1[t][128:256, :], cin1[t], dsem("fcc1b" + nm), reads=[bf("cin1" + nm)], writes=[bf("cout1" + nm)])
            else:
                pool_cc(I("collective_compute", kind="AllGather", op=ALU.bypass, replica_groups=[[0, 1], [2, 3], [4, 5], [6, 7]],
                          ins=[cin1[t].opt()], outs=[cout1[t].opt()]), reads=[bf("cin1" + nm)], writes=[bf("cout1" + nm)])
            mv = mod_all[:, 2 * lo:2 * lo + 2 * n].rearrange("p (r f) -> p r f", r=2)
            bdma(mv, cout1[t].rearrange("(r p) f -> p r f", p=128), dsem("mall" + nm), reads=[bf("cout1" + nm)], writes=[bf("mod_all" + nm)])
            return mv
        mva = ada_gather(0, 0, 16)
        prefetch(10)
        dve(I("tensor_copy", out=sel[:, 0:32].rearrange("p (a r i) -> p a r i", a=2, r=2), in_=mva.rearrange("p r (a i) -> p a r i", a=2)),
            reads=[bf("mod_alla")], writes=[bf("sel")])
        dve(I("scalar_tensor_tensor", out=Gs, in0=sel[:, 16:32], scalar=1.0, in1=P(C_NG, 16), op0=ALU.add, op1=ALU.mult),
            reads=[bf("sel"), bf("pp")], writes=[bf("Gs")])
        f_transposes(0)
        f_transposes(1)
        alias("xnh", ["xn0"])
        f_scale(8)
        items = [("transpose", dict(out=tph[:, j, :], in_=xn[0][0:HALO, 0, j * 128:(j + 1) * 128], identity=ident_b[0:HALO, 0:HALO]))
                 for j in range(KC)]
        pe(G(items), reads=[bf("xnh"), bf("ident_b")], writes=[bf("bank4")])
        for j in range(KC):
            dve(I("tensor_scalar", out=hT[:, j, T:TW], in0=tph[:, j, :], scalar1=Gs[:, j:j + 1], scalar2=shift[:, j:j + 1], op0=ALU.mult, op1=ALU.add),
                reads=[bf("bank4"), bf("Gs"), bf("sel")], writes=[bf("hT")])
        dump("hT", hT, [bf("hT")])

        front_bufs = ["xr0", "xr1", "xr2", "junk", "xnh"] + ["xn%d" % i for i in range(8)]
        for c in range(9):
            alias("UL%d" % c, front_bufs)
        for c in range(8):
            alias("UP%d" % c, front_bufs)
        for c in range(1, 9):
            dve(I("memset", ap=ULuc[:, c, 0:2], constant=0.0), writes=[bf("UL%d" % c)])
        for c in range(8):
            dve(I("memset", ap=UP[:, c, 0:16], constant=0.0), writes=[bf("UP%d" % c)])
        dve(I("memset", ap=tmpA[:, 0:16], constant=0.0), writes=[bf("tmpA")])
        dve(I("memset", ap=tmpB[:, 0:16], constant=0.0), writes=[bf("tmpB")])

        S_.marks.append(("inproj", S_.nops))
        for c in range(16):
            alias("SG%d" % c, ["stage0", "stage1"])
        psbuf = [bank(0, 2), bank(2, 2)]
        pshq = bank(4)[:, 0:32].rearrange("p (m t) -> p m t", m=4)
        alias("pshq", ["bank4"])
        mcount = [0]
        def inproj_piece(q):
            rslot, rname = slot_of[("win", q)]
            is_u = q < 8
            for mc in range(2):
                zc = PIECE_COLS[q] // 128 + mc
                b_ = mcount[0] % 2; mcount[0] += 1
                items = []
                for k in range(KC):
                    lhsT = rslot[:, k, mc * 128:(mc + 1) * 128]
                    items.append(("matmul", dict(out=psbuf[b_][:, 0:512], lhsT=lhsT, rhs=hT[:, k, 0:512], start=(k == 0), stop=(k == KC - 1))))
                    items.append(("matmul", dict(out=psbuf[b_][:, 512:1024], lhsT=lhsT, rhs=hT[:, k, 512:1024], start=(k == 0), stop=(k == KC - 1))))
                pe(G(items), reads=[bf(rname), bf("hT")], writes=[bf("psb%d" % b_)])
                bias = P(C_BIN + zc)
                if zc < 8:
                    act(I("activation", out=UP[:, zc, 16:16 + T], in_=psbuf[b_], func=AF.Identity, bias=bias),
                        reads=[bf("psb%d" % b_), bf("pp")], writes=[bf("UP%d" % zc)])
                elif zc < 16:
                    c = zc - 8
                    act(I("activation", out=ULuc[:, c + 1, 2:2 + T], in_=psbuf[b_], func=AF.Identity, bias=bias),
                        reads=[bf("psb%d" % b_), bf("pp")], writes=[bf("UL%d" % (c + 1))])
                else:
                    c = zc - 16
                    act(I("activation", out=SG[:, c, :], in_=psbuf[b_], func=AF.Silu, bias=bias),
                        reads=[bf("psb%d" % b_), bf("pp")], writes=[bf("SG%d" % c)])
            if is_u:
                items = []
                for mc in range(2):
                    for k in range(KC):
                        items.append(("matmul", dict(out=pshq[:, mc, :], lhsT=rslot[:, k, mc * 128:(mc + 1) * 128], rhs=hT[:, k, T:TW],
                                                     start=(k == 0), stop=(k == KC - 1))))
                pe(G(items), reads=[bf(rname), bf("hT")], writes=[bf("pshq")])
                for mc in range(2):
                    zc = PIECE_COLS[q] // 128 + mc
                    bias = P(C_BIN + zc)
                    if zc < 8:
                        act(I("activation", out=UP[:, zc, 16 + T:16 + TW], in_=pshq[:, mc, :], func=AF.Identity, bias=bias),
                            reads=[bf("pshq"), bf("pp")], writes=[bf("UP%d" % zc)])
                    else:
                        c = zc - 8
                        act(I("activation", out=ULuc[:, c + 1, 2 + T:2 + TW], in_=pshq[:, mc, :], func=AF.Identity, bias=bias),
                            reads=[bf("pshq"), bf("pp")], writes=[bf("UL%d" % (c + 1))])

        def pool_mixer(g):
            w = WINS[g]; hi = w // 2 - 1
            pooled_g = pooled2[g % 2]; pname = "pooled%d" % (g % 2)
            bufs = [(tmpA, "tmpA"), (tmpB, "tmpB")]
            for kk in range(2):
                c = 2 * g + kk
                U = UP[:, c, :]
                sh = 1; lvl = 0
                cur = curname = None
                while sh < w:
                    dst, dname = bufs[lvl % 2]
                    if lvl == 0:
                        dve(I("tensor_tensor", out=dst[:, 16:16 + TW], in0=U[:, 16:16 + TW], in1=U[:, 15:15 + TW], op=ALU.add),
                            reads=[bf("UP%d" % c)], writes=[bf(dname)])
                    else:
                        srcb, sname = bufs[(lvl - 1) % 2]
                        dve(I("tensor_tensor", out=dst[:, 16:16 + TW], in0=srcb[:, 16:16 + TW], in1=srcb[:, 16 - sh:16 - sh + TW], op=ALU.add),
                            reads=[bf(sname)], writes=[bf(dname)])
                    cur, curname = dst, dname
                    sh *= 2; lvl += 1
                Dm = cur
                dve(I("scalar_tensor_tensor", out=ttmp, in0=Dm[:, 16 + hi:16 + hi + T], scalar=P(C_AW + g), in1=U[:, 16:16 + T],
                      op0=ALU.mult, op1=ALU.subtract),
                    reads=[bf(curname), bf("UP%d" % c), bf("pp")], writes=[bf("ttmp")])
                dve(I("scalar_tensor_tensor", out=pooled_g[:, kk, :], in0=Dm[:, 17 + hi:17 + hi + T], scalar=P(C_BW + g), in1=ttmp,
                      op0=ALU.mult, op1=ALU.add),
                    reads=[bf(curname), bf("ttmp")], writes=[bf(pname)])
                t8 = ttmp[:, 0:8]
                dve(I("tensor_scalar", out=t8, in0=Dm[:, 16 + hi:24 + hi], scalar1=P(C_AL), scalar2=None, op0=ALU.mult),
                    reads=[bf(curname)], writes=[bf("ttmp")])
                dve(I("scalar_tensor_tensor", out=t8, in0=Dm[:, 17 + hi:25 + hi], scalar=P(C_BE), in1=t8, op0=ALU.mult, op1=ALU.add),
                    reads=[bf(curname)], writes=[bf("ttmp")])
                dve(I("tensor_tensor", out=t8, in0=t8, in1=P(C_INVC + 8 * g, 8), op=ALU.mult), writes=[bf("ttmp")])
                dve(I("tensor_tensor", out=pooled_g[:, kk, 0:8], in0=t8, in1=U[:, 16:24], op=ALU.subtract),
                    reads=[bf("UP%d" % c), bf("ttmp")], writes=[bf(pname)])
            dump("pooled%d" % g, pooled_g, [bf(pname)])

        ppb = [bank(5), bank(6)]
        stP = [bank(6), bank(7)]
        ACC = Tb[6]
        pooled2 = [pooled, Tb[7].bitcast(BF16).rearrange("p (k t) -> p k t", k=2)]
        ppc = [0]
        def pool_matmul(g):
            for qc in range(2):
                c = 2 * g + qc
                for n in range(2):
                    pb = ppc[0] % 2; ppc[0] += 1
                    items = [("matmul", dict(out=ppb[pb], lhsT=WPl[:, g, kk, qc * 128:(qc + 1) * 128], rhs=pooled2[g % 2][:, kk, n * 512:(n + 1) * 512],
                                             start=(kk == 0), stop=(kk == 1))) for kk in range(2)]
                    pe(G(items), reads=[bf("pooled%d" % (g % 2)), bf("WPl")], writes=[bf("ppb%d" % pb)])
                    act(I("activation", out=SQ[:, n * 512:(n + 1) * 512], in_=ppb[pb], func=AF.Square, scale=P(C_PS + c), bias=P(C_S3 + c)),
                        reads=[bf("ppb%d" % pb), bf("pp")], writes=[bf("SQ%d" % n)])
                    act(I("activation", out=UP[:, c, 16 + n * 512:16 + (n + 1) * 512], in_=ppb[pb], func=AF.Identity, scale=P(C_S1 + c), bias=P(C_S2 + c)),
                        reads=[bf("ppb%d" % pb)], writes=[bf("UP%d" % c)])
                    if c == 0:
                        pool(I("tensor_copy", out=ACC[:, n * 512:(n + 1) * 512], in_=SQ[:, n * 512:(n + 1) * 512]), reads=[bf("SQ%d" % n)],
                             writes=[bf("ACC%d" % n)])
                    else:
                        pool(I("tensor_tensor", out=ACC[:, n * 512:(n + 1) * 512], in0=ACC[:, n * 512:(n + 1) * 512], in1=SQ[:, n * 512:(n + 1) * 512],
                               op=ALU.add), reads=[bf("SQ%d" % n)], writes=[bf("ACC%d" % n)])
        def pool_stats():
            alias("stP0", ["ppb1"]); alias("stP1", ["bank7"])
            for n in range(2):
                pe(I("matmul", out=stP[n], lhsT=ones_f, rhs=ACC[:, n * 512:(n + 1) * 512], start=True, stop=True),
                   reads=[bf("ACC%d" % n), bf("ones_f")], writes=[bf("stP%d" % n)])

        def rstd_bc_from(stbanks, stfmt, nch):
            for n in range(2):
                act(I("activation", out=RS[:, n * 512:(n + 1) * 512], in_=stbanks[n], func=AF.Sqrt, scale=1.0 / nch, bias=EPS),
                    reads=[bf(stfmt % n)], writes=[bf("RS")])
            dve(I("reciprocal", out=RS, in_=RS), writes=[bf("RS")])

        def conv_chunk(c):
            src = ULuc[:, c + 1, :]; dst = ULuc[:, c, 0:T]
            dve(I("tensor_scalar", out=dst, in0=src[:, 0:T], scalar1=P(C_CW + c), scalar2=P(C_CB + c), op0=ALU.mult, op1=ALU.add),
                reads=[bf("UL%d" % (c + 1)), bf("pp")], writes=[bf("UL%d" % c)])
            for j in range(1, 5):
                dve(I("scalar_tensor_tensor", out=dst, in0=src[:, j:j + T], scalar=P(C_CW + 8 * j + c), in1=dst, op0=ALU.mult, op1=ALU.add),
                    reads=[bf("UL%d" % (c + 1))], writes=[bf("UL%d" % c)])
            dve(I("tensor_copy", out=ucb[:, c, :], in_=dst), reads=[bf("UL%d" % c)], writes=[bf("ucb%d" % c)])

        def pool_gate(c):
            dve(I("tensor_tensor", out=UP[:, c, 16:16 + T], in0=UP[:, c, 16:16 + T], in1=SG[:, c, :], op=ALU.mult),
                reads=[bf("SG%d" % c)], writes=[bf("UP%d" % c)])
            dve(I("tensor_tensor", out=SG[:, c, :], in0=UP[:, c, 16:16 + T], in1=RS, op=ALU.mult),
                reads=[bf("UP%d" % c), bf("RS")], writes=[bf("SG%d" % c)])

        seq_pos = [8]
        def run_seq_until_win(q):
            while True:
                kind, qq = SEQ[seq_pos[0]]
                if kind == "ada":
                    if qq == 8:
                        alias("bank7", ["stP1"])
                    ada_piece(qq, slot_of[(kind, qq)])
                    if qq == 11:
                        mvb = ada_gather(1, 16, 8)
                        dve(I("tensor_copy", out=sel[:, 32:48].rearrange("p (r i) -> p r i", r=2), in_=mvb), reads=[bf("mod_allb")], writes=[bf("selg")])
                else:
                    inproj_piece(qq)
                prefetch(seq_pos[0] + 3)
                seq_pos[0] += 1
                if kind == "win" and qq == q:
                    break
        wob_sem = dsem("wobf")
        wgb_sem = dsem("wgbf")
        wgb_dst = wg_bf.rearrange("d h c e -> (d h c) e"); wgb_src = wg_d.rearrange("d h c e -> (d h c) e")
        for q in range(8):
            run_seq_until_win(2 * q + 1)
            pool_dma(wo_bf[q * 256:(q + 1) * 256, :], wo_d[q * 256:(q + 1) * 256, :], wob_sem, writes=[bf("wo_bf")])
            pool_dma(wgb_dst[q * 256:(q + 1) * 256, :], wgb_src[q * 256:(q + 1) * 256, :], wgb_sem, writes=[bf("wg_bf")])
            if q == 0:
                pool_mixer(0)
            if q == 1:
                alias("ppb0", ["tp0"]); alias("ppb1", ["tp1"])
                pool_matmul(0)
                pool_mixer(1); pool_mixer(2)
            if q == 2:
                pool_matmul(1); pool_matmul(2)
                pool_mixer(3)
            if q == 3:
                pool_matmul(3)
                pool_stats()
                rstd_bc_from(stP, "stP%d", 1024)
                for c in range(8):
                    alias("ucb%d" % c, ["tmpA", "tmpB", "ttmp", "pooled0", "pooled1"])
                for c in range(0, 8):
                    conv_chunk(c)
            if q == 5:
                for c in range(8):
                    pool_gate(c)
        dump("UP", UP, bfs("UP%d", range(8)))
        dump("uc", ULuc, bfs("UL%d", range(9)))

        gps = bank(6)
        alias("gps", ["stP0", "ppb1", "tp1"])
        alias("gate_bc", ["ring2"])
        for j in range(KC):
            dve(I("tensor_scalar", out=diag, in0=ident_f, scalar1=gate_fm[:, j:j + 1], scalar2=None, op0=ALU.mult),
                reads=[bf("ident_f"), bf("selg")], writes=[bf("diag")])
            pe(I("matmul", out=gps[:, 0:128], lhsT=ones_f, rhs=diag, start=True, stop=True), reads=[bf("diag"), bf("ones_f")],
               writes=[bf("gps")])
            dve(I("tensor_copy", out=gate_bc[:, j * 128:(j + 1) * 128], in_=gps[:, 0:128]), reads=[bf("gps")], writes=[bf("gate_bc")])
        S_.marks.append(("lru", S_.nops))
        lru_old = ["hT", "ring0", "ring1"]
        up_old = ["UP%d" % c for c in range(8)]
        for i in range(6):
            alias("T%d" % i, up_old)
        for i in range(9, 12):
            alias("T%d" % i, lru_old)
        for c in range(8):
            alias("HF%d" % c, lru_old)
        for s in range(3):
            alias("wgr%d" % s, up_old)
        wg_sem = [dsem("wg0"), dsem("wg1"), dsem("wg2")]
        wg_v = wg_bf.rearrange("d h (k p) n -> p d h k n", p=128)
        def load_wg(u):
            d_, h = u // 4, u % 4
            s = u % 3
            sp_dma(wgr[s], wg_v[:, d_, h, :, :], wg_sem[s], reads=[bf("wg_bf")], writes=[bf("wgr%d" % s)])
        for u in range(3):
            load_wg(u)
        UA = [bank(0, 4), bank(4, 4)]
        alias("UB0", ["psb0", "psb1"]); alias("UB1", ["pshq", "stP0", "stP1", "ppb0", "ppb1", "tp0", "tp1", "bank7", "bank4", "gps"])
        alias("SQ", ["SQ0", "SQ1"]); alias("T6", ["ACC0", "ACC1"]); alias("T7", ["pooled1"])
        ones_bc = P(C_HALF).to_broadcast([128, T])
        units = [(d_, c) for d_ in range(2) for c in range(8)]
        def unit_bufs(ui):
            ub = ui % 2
            ts = ui % 4
            return ub, (Tb[3 * ts], Tb[3 * ts + 1], Tb[3 * ts + 2]), ("T%d" % (3 * ts), "T%d" % (3 * ts + 1), "T%d" % (3 * ts + 2))
        def lru_pe(ui):
            d_, c = units[ui]
            h, cl = c // 2, c % 2
            u = d_ * 4 + h
            ws = u % 3
            ub, _, _ = unit_bufs(ui)
            pr = UA[ub][:, 0:1024]; pi = UA[ub][:, 1024:2048]
            items = []
            for (dst, ec) in ((pr, cl), (pi, 2 + cl)):
                for n in range(2):
                    for kk in range(2):
                        items.append(("matmul", dict(out=dst[:, n * 512:(n + 1) * 512], lhsT=wgr[ws][:, kk, ec * 128:(ec + 1) * 128],
                                                     rhs=ucb[:, 2 * h + kk, n * 512:(n + 1) * 512], start=(kk == 0), stop=(kk == 1))))
            pe(G(items), reads=[bf("wgr%d" % ws), bf("ucb%d" % (2 * h)), bf("ucb%d" % (2 * h + 1))], writes=[bf("UB%d" % ub)])
            if cl == 1 and u + 3 < 8:
                load_wg(u + 3)
        def lru_actA(ui):
            d_, c = units[ui]
            h, cl = c // 2, c % 2
            ub, (T1, T2, T3), (n1, n2, n3) = unit_bufs(ui)
            pr = UA[ub][:, 0:1024]; pi = UA[ub][:, 1024:2048]
            bgr = P(C_BGH + d_ * 16 + h * 4 + cl); bgi = P(C_BGH + d_ * 16 + h * 4 + 2 + cl)
            lc = d_ * 8 + c
            act(I("activation", out=T1, in_=pr, func=AF.Tanh, scale=0.5, bias=bgr), reads=[bf("UB%d" % ub), bf("pp")], writes=[bf(n1)])
            act(I("activation", out=T2, in_=pi, func=AF.Tanh, scale=0.5, bias=bgi), reads=[bf("UB%d" % ub)], writes=[bf(n2)])
            act(I("activation", out=T3, in_=T1, func=AF.Exp, scale=P(C_C1H + lc), bias=P(C_C1H + lc)), reads=[bf(n1)], writes=[bf(n3)])
            if d_ == 0:
                dve(I("tensor_tensor", out=T1, in0=T3, in1=T3, op=ALU.mult), reads=[bf(n3)], writes=[bf(n1)])
            else:
                act(I("activation", out=T1, in_=T3, func=AF.Square), reads=[bf(n3)], writes=[bf(n1)])
            act(I("activation", out=T2, in_=T2, func=AF.Identity, bias=1.0), writes=[bf(n2)])
            pool(I("tensor_tensor", out=T2, in0=T2, in1=ULuc[:, c, 0:T], op=ALU.mult), reads=[bf("UL%d" % c)], writes=[bf(n2)])
        def lru_actS(ui):
            d_, c = units[ui]
            ub, (T1, T2, T3), (n1, n2, n3) = unit_bufs(ui)
            act(I("activation", out=T1, in_=T1, func=AF.Sqrt, scale=-0.25, bias=0.25), writes=[bf(n1)])
            dve(I("tensor_tensor", out=T2, in0=T2, in1=T1, op=ALU.mult), reads=[bf(n1)], writes=[bf(n2)])
            if d_ == 0:
                dve(I("tensor_tensor_scan", out=HF[:, c, :], data0=T3, data1=T2, initial=0.0, op0=ALU.mult, op1=ALU.add),
                    reads=[bf(n3), bf(n2)], writes=[bf("HF%d" % c)])
            else:
                dve(I("tensor_tensor_scan", out=T1[:, ::-1], data0=T3[:, ::-1], data1=T2[:, ::-1], initial=carry[:, c:c + 1],
                      op0=ALU.mult, op1=ALU.add),
                    reads=[bf(n3), bf(n2), bf("carry%d" % (c // 4))], writes=[bf(n1)])
                dve(I("tensor_tensor", out=HF[:, c, :], in0=HF[:, c, :], in1=T1, op=ALU.add), reads=[bf(n1)], writes=[bf("HF%d" % c)])
                act(I("activation", out=SQ, in_=HF[:, c, :], func=AF.Square), reads=[bf("HF%d" % c)], writes=[bf("SQ")])
                if c == 0:
                    dve(I("tensor_copy", out=RS, in_=SQ), reads=[bf("SQ")], writes=[bf("RS")])
                else:
                    pool(I("tensor_tensor", out=RS, in0=RS, in1=SQ, op=ALU.add), reads=[bf("SQ")], writes=[bf("RS")])
                dve(I("scalar_tensor_tensor", out=HF[:, c, :], in0=HF[:, c, :], scalar=P(C_GNL + c), in1=SG[:, 8 + c, :], op0=ALU.mult, op1=ALU.mult),
                    reads=[bf("SG%d" % (8 + c)), bf("pp")], writes=[bf("HF%d" % c)])
        def lru_pairs(d_, pairs):
            base = d_ * 8
            us = [base + 2 * pair + k for pair in pairs for k in range(2)]
            lru_pe(us[0]); lru_pe(us[1])
            for i, u in enumerate(us):
                lru_actA(u)
                if i + 2 < len(us):
                    lru_pe(us[i + 2])
            for u in us:
                lru_actS(u)
        def carry_exchange(t):
            nm = "ab"[t]
            dve(I("tensor_copy", out=cko[:, 8 * t:8 * t + 8], in_=HF[:, :, T - 1]), reads=bfs("HF%d", range(4 * t, 4 * t + 4)), writes=[bf("cko" + nm)])
            sp_dma(cin2[t], cko[:, 8 * t:8 * t + 8], dsem("cin2" + nm), reads=[bf("cko" + nm)], writes=[bf("cin2" + nm)])
            if fake_cc:
                pool_dma(cout2[t][0:128, :], cin2[t], dsem("fcc2" + nm), reads=[bf("cin2" + nm)], writes=[bf("cout2" + nm)])
            else:
                pool_cc(I("collective_compute", kind="AllGather", op=ALU.bypass, replica_groups=[[0, 1], [2, 3], [4, 5], [6, 7]],
                          ins=[cin2[t].opt()], outs=[cout2[t].opt()]), reads=[bf("cin2" + nm)], writes=[bf("cout2" + nm)])
            ckv = ckall[:, 16 * t:16 * t + 16].rearrange("p (r f) -> p r f", r=2)
            sp_dma(ckv, cout2[t].rearrange("(r p) f -> p r f", p=128), dsem("ckall" + nm), reads=[bf("cout2" + nm)], writes=[bf("ckall" + nm)])
            cs = slice(4 * t, 4 * t + 4)
            dve(I("tensor_scalar", out=carry[:, cs], in0=ckv[:, 0, cs], scalar1=P(C_OHP), scalar2=None, op0=ALU.mult),
                reads=[bf("ckall" + nm), bf("pp")], writes=[bf("carry%d" % t)])
            dve(I("scalar_tensor_tensor", out=carry[:, cs], in0=ckv[:, 1, cs], scalar=P(C_OHP + 1), in1=carry[:, cs], op0=ALU.mult, op1=ALU.add),
                reads=[bf("ckall" + nm)], writes=[bf("carry%d" % t)])

        S_.marks.append(("cc2", S_.nops))
        lru_pairs(0, [0, 1])
        carry_exchange(0)
        lru_pairs(0, [2, 3])
        carry_exchange(1)
        dump("hf", HF, bfs("HF%d", range(8)))
        lru_pairs(1, [0, 1])
        lru_pairs(1, [2])
        lru_pairs(1, [3])
        dump("ylru", HF, bfs("HF%d", range(8)))

        stL = [bank(0), bank(1)]
        alias("stL0", ["UB0"]); alias("stL1", ["UB0"])
        for n in range(2):
            pe(I("matmul", out=stL[n], lhsT=ones_f, rhs=RS[:, n * 512:(n + 1) * 512], start=True, stop=True),
               reads=[bf("RS"), bf("ones_f")], writes=[bf("stL%d" % n)])
        rstd_bc_from(stL, "stL%d", 1024)
        for c in range(8):
            dve(I("tensor_tensor", out=SG[:, 8 + c, :], in0=HF[:, c, :], in1=RS, op=ALU.mult),
                reads=[bf("HF%d" % c), bf("RS")], writes=[bf("SG%d" % (8 + c))])
        dump("y", SG, bfs("SG%d", range(16)))

        S_.marks.append(("tail", S_.nops))
        alias("woutA", ["UL%d" % c for c in range(9)])
        alias("woutB", ["T%d" % i for i in range(6)] + ["wgr0", "wgr1", "wgr2"])
        wo_v = wo_bf.rearrange("(k p) n -> p k n", p=128)
        sp_dma(woutB[:, 0:4, :], wo_v[:, 8:12, :], dsem("woB0"), reads=[bf("wo_bf")], writes=[bf("woutB")])
        sp_dma(woutB[:, 4:8, :], wo_v[:, 12:16, :], dsem("woB1"), reads=[bf("wo_bf")], writes=[bf("woutB")])
        sp_dma(woutA[:, 0:4, :], wo_v[:, 0:4, :], dsem("woA0"), reads=[bf("wo_bf")], writes=[bf("woutA")])
        sp_dma(woutA[:, 4:8, :], wo_v[:, 4:8, :], dsem("woA1"), reads=[bf("wo_bf")], writes=[bf("woutA")])

        pm_old = ["tmpA", "tmpB", "ttmp", "pooled0", "pooled1"] + ["ucb%d" % c for c in range(8)]
        alias("fg_bc", pm_old)
        alias("gb_bc", ["SQ", "RS"])
        for s in range(2):
            alias("xt%d" % s, lru_old + ["T9", "T10", "T11"])
            alias("ot%d" % s, ["HF%d" % c for c in range(8)])
        sp_dma(fg_bc, fg_d.partition_broadcast(128), dsem("fg"), writes=[bf("fg_bc")])
        sp_dma(ot[0], bo_d.partition_broadcast(128), dsem("bo"), writes=[bf("ot0")])
        dve(I("tensor_tensor", out=gb_bc, in0=gate_bc, in1=ot[0], op=ALU.mult), reads=[bf("gate_bc"), bf("ot0")], writes=[bf("gb_bc")])

        xt_sem = [dsem("xt0"), dsem("xt1")]
        ot_sem = [dsem("ot0"), dsem("ot1")]
        OB = [bank(0, 4), bank(4, 4)]
        alias("OB0", ["stL0", "stL1", "UB0"]); alias("OB1", ["UB1"])
        out_toks = []
        def tail_load(i):
            s = i % 2
            sp_dma(xt[s], x_d[i * 128:(i + 1) * 128, :], xt_sem[s], writes=[bf("xt%d" % s)])
        tail_load(0); tail_load(1)
        for i in range(8):
            s = i % 2
            pool(I("tensor_tensor", out=xt[s], in0=xt[s], in1=gb_bc, op=ALU.add), reads=[bf("gb_bc")], writes=[bf("xt%d" % s)])
            items = []
            for n in range(4):
                for k in range(KC):
                    w_ = woutA if k < 8 else woutB
                    items.append(("matmul", dict(out=OB[s][:, n * 512:(n + 1) * 512], lhsT=SG[:, k, i * 128:(i + 1) * 128],
                                                 rhs=w_[:, k % 8, n * 512:(n + 1) * 512], start=(k == 0), stop=(k == KC - 1))))
            pe(G(items), reads=bfs("SG%d", range(16)) + [bf("woutA"), bf("woutB")], writes=[bf("OB%d" % s)])
            dve(I("tensor_tensor", out=ot[s], in0=OB[s], in1=gate_bc, op=ALU.mult), reads=[bf("OB%d" % s), bf("gate_bc")],
                writes=[bf("ot%d" % s)])
            dve(I("tensor_tensor", out=ot[s], in0=ot[s], in1=xt[s], op=ALU.add), reads=[bf("xt%d" % s)], writes=[bf("ot%d" % s)])
            act(I("activation", out=xt[s], in_=ot[s], func=AF.Square, accum_out=ss2[:, i:i + 1]), reads=[bf("ot%d" % s)],
                writes=[bf("xt%d" % s), bf("ss2_%d" % i)])
            act(I("activation", out=sq2[:, i:i + 1], in_=ss2[:, i:i + 1], func=AF.Sqrt, scale=1.0 / D, bias=EPS),
                reads=[bf("ss2_%d" % i)], writes=[bf("sq2_%d" % i)])
            dve(I("reciprocal", out=rstd2[:, i:i + 1], in_=sq2[:, i:i + 1]), reads=[bf("sq2_%d" % i)], writes=[bf("rstd2_%d" % i)])
            dve(I("scalar_tensor_tensor", out=ot[s], in0=ot[s], scalar=rstd2[:, i:i + 1], in1=fg_bc, op0=ALU.mult, op1=ALU.mult),
                reads=[bf("rstd2_%d" % i), bf("fg_bc")], writes=[bf("ot%d" % s)])
            out_toks.append(sp_dma(out_d[i * 128:(i + 1) * 128, :], ot[s], ot_sem[s], reads=[bf("ot%d" % s)]))
            if i + 2 < 8:
                tail_load(i + 2)
        final_toks = [t for t in out_toks[-2:] if t is not None] + ([(dbg_sem[0], dbg_sem[1])] if dbg_sem[1] else [])
        if limit is not None:
            final_toks = [(d[0], d[1]) for d in S_.dma_sems if d[1] > 0] + [(S_.sem[e], S_.cnt[e]) for e in S_.sem if S_.cnt[e] > 0]
        build_program.marks = list(S_.marks) + [("end", S_.nops)]

        with nc.Block() as block:
            @block.sync
            def _(e):
                S_.emit("sp", e)
                for (s, val) in final_toks:
                    e.wait_ge(s, val)
            @block.scalar
            def _(e):
                S_.emit("act", e)
            @block.vector
            def _(e):
                S_.emit("dve", e)
            @block.gpsimd
            def _(e):
                S_.emit("pool", e)
            @block.tensor
            def _(e):
                S_.emit("pe", e)
    return nc


def _fm(vec):
    vec = np.asarray(vec, np.float32)
    return np.ascontiguousarray(vec.reshape(-1, 128).T)


def _prep_core(c, I):
    b, half = c // 2, c % 2
    f32 = np.float32
    x = I["x"][b]
    if half == 0:
        x_loc = x[0:TW]
        dirs = (0, 1)
    else:
        x_loc = x[::-1][0:TW]
        dirs = (1, 0)
    pp = np.zeros((128, NP), f32)
    pp[:, C_NG:C_NG + 16] = _fm(I["norm_g"][0])
    pp[:, C_BIN:C_BIN + 32] = _fm(I["b_in"][0])
    cw = I["conv_w"][0]
    z = np.zeros_like(cw[0])
    taps = [cw[0], cw[1], cw[2], cw[3], z] if half == 0 else [z, cw[3], cw[2], cw[1], cw[0]]
    for j in range(5):
        pp[:, C_CW + 8 * j:C_CW + 8 * j + 8] = _fm(taps[j])
    pp[:, C_CB:C_CB + 8] = _fm(I["conv_b"][0])
    for ld in range(2):
        pp[:, C_BG + 16 * ld:C_BG + 16 * ld + 16] = _fm(I["b_gate"][0, dirs[ld]].reshape(-1))
        pp[:, C_LAM + 8 * ld:C_LAM + 8 * ld + 8] = _fm(I["lru_lambda"][0, dirs[ld]])
    pp[:, C_BP:C_BP + 8] = _fm(I["b_pool"][0].reshape(-1))
    pp[:, C_PS:C_PS + 8] = _fm(I["pool_scale"][0])
    pp[:, C_GNP:C_GNP + 8] = _fm(I["out_norm_pool_g"][0])
    pp[:, C_GNL:C_GNL + 8] = _fm(I["out_norm_lru_g"][0])
    pp[:, C_OH + b] = 1.0
    al, be = (1.0, 0.0) if half == 0 else (0.0, 1.0)
    for g, w in enumerate(WINS):
        pp[:, C_AW + g] = al / w
        pp[:, C_BW + g] = be / w
        hi, lo = w // 2 - 1, w // 2
        p = np.arange(8)
        cnt = np.minimum(w, p + (hi if half == 0 else lo) + 1)
        pp[:, C_INVC + 8 * g:C_INVC + 8 * g + 8] = (1.0 / cnt).astype(f32)[None, :]
    pp[:, C_AL] = al
    pp[:, C_BE] = be
    pp[:, C_OHP + (1 - half)] = 1.0
    ada_cols = np.concatenate([part * D + half * 1024 + np.arange(1024) for part in range(3)])
    pp[:, C_BADA:C_BADA + 24] = _fm(I["b_ada"][0][ada_cols])
    m = {
        "x_loc": np.ascontiguousarray(x_loc, f32),
        "c_own": np.ascontiguousarray(I["c"][b].reshape(128, 16), f32),
        "w_ada": np.ascontiguousarray(I["w_ada"][0][:, ada_cols], f32),
        "pp": pp,
        "w_in": np.ascontiguousarray(I["w_in"][0], f32),
        "w_gate": np.ascontiguousarray(I["w_gate"][0][list(dirs)], f32),
        "w_pool": np.ascontiguousarray(I["w_pool"][0], f32),
        "w_out": np.ascontiguousarray(I["w_out"][0], f32),
        "ident": np.eye(128, dtype=f32),
        "fg": np.ascontiguousarray(I["final_norm_g"], f32),
        "bo": np.ascontiguousarray(I["b_out"][0], f32),
    }
    return m


_NC_CACHE = {}


def kernel(**inputs):
    I = {k: np.asarray(v) for k, v in inputs.items()}
    if "nc" not in _NC_CACHE:
        _NC_CACHE["nc"] = build_program()
    nc = _NC_CACHE["nc"]
    in_maps = [_prep_core(c, I) for c in range(NCORES)]
    res = run_bass_kernel_spmd(nc, in_maps, core_ids=list(range(NCORES)))
    out = np.empty((4, S, D), np.float32)
    for c in range(NCORES):
        b, half = c // 2, c % 2
        o = np.asarray(res.results[c]["out_loc"], np.float32)
        if half == 0:
            out[b, 0:T] = o
        else:
            out[b, T:S] = o[::-1]
    return out
```

```python
import math
import numpy as np
import concourse.bass as bass
import concourse.mybir as mybir
from concourse.bass_utils import run_bass_kernel_spmd

F32 = mybir.dt.float32
BF16 = mybir.dt.bfloat16
U8 = mybir.dt.uint8
AF = mybir.ActivationFunctionType
ALU = mybir.AluOpType

NCORES = 8
D = 2048
S = 2048
T = 1024
HALO = 8
TW = T + HALO
KC = 16
EPS = 1e-6
WINS = (2, 4, 8, 16)

_c = 0
def _col(n):
    global _c
    r = _c
    _c += n
    return r
C_NG = _col(16); C_BIN = _col(32); C_CW = _col(40); C_CB = _col(8); C_BG = _col(64); C_LAM = _col(16)
C_BP = _col(8); C_PS = _col(8); C_GNP = _col(8); C_GNL = _col(8); C_OH = _col(4); C_AW = _col(4); C_BW = _col(4)
C_AL = _col(1); C_BE = _col(1); C_OHP = _col(8); C_BADA = _col(24); C_INVC = _col(32)
NP = _c
C_BGH = _col(64); C_C1H = _col(16); C_C1 = _col(16); C_C1Q = _col(16); C_S1 = _col(8); C_S2 = _col(8); C_S3 = _col(8)
C_TMP = _col(16 * 6); C_HALF = _col(1); C_MHALF = _col(1)
NPD = _c

PW = 256
PIECE_COLS = [i * PW for i in range(16)]


class Buf:
    __slots__ = ("w", "r")
    def __init__(self):
        self.w = []
        self.r = []


class Sched:
    def __init__(self, nc, eng_sems):
        self.nc = nc
        self.ops = {e: [] for e in eng_sems}
        self.sem = eng_sems
        self.cnt = {e: 0 for e in eng_sems}
        self.waited = {e: {} for e in eng_sems}
        self.nops = 0
        self.limit = None
        self.marks = []
        self.dma_sems = []

    def _waits(self, eng, toks):
        best = {}
        for (s, v) in toks:
            k = id(s)
            if k not in best or best[k][1] < v:
                best[k] = (s, v)
        out = []
        for k, (s, v) in best.items():
            if self.waited[eng].get(k, 0) >= v:
                continue
            self.waited[eng][k] = v
            out.append((s, v))
        return out

    def op(self, eng, fn, reads=(), writes=(), extra=(), dma_sem=None, sig_eng=None):
        self.nops += 1
        if self.limit is not None and self.nops > self.limit:
            return None
        toks = list(extra)
        for b in reads:
            toks += b.w
        for b in writes:
            toks += b.w + b.r
        waits = self._waits(eng, toks)
        if dma_sem is not None:
            dma_sem[1] += 16
            tok = (dma_sem[0], dma_sem[1])
            self.ops[eng].append((fn, waits, ("dma", dma_sem[0])))
        else:
            se = sig_eng or eng
            self.cnt[se] += 1
            tok = (self.sem[se], self.cnt[se])
            self.ops[eng].append((fn, waits, ("inc", self.sem[se])))
        for b in reads:
            b.r.append(tok)
        for b in writes:
            b.w = [tok]
            b.r = []
        return tok

    def emit(self, eng, e):
        for fn, waits, sig in self.ops[eng]:
            for (s, v) in waits:
                e.wait_ge(s, v)
            ins = fn(e)
            if sig[0] == "dma":
                ins.then_inc(sig[1], 16)
            else:
                ins.then_inc(sig[1], 1)


def I(method, **kw):
    return lambda e: getattr(e, method)(**kw)


def G(items):
    def fn(e):
        ins = None
        for (m, kw) in items:
            ins = getattr(e, m)(**kw)
        return ins
    return fn


def build_program(debug=(), limit=None, fake_cc=False, shrink_test=False):
    nc = bass.Bass("TRN2", target_bir_lowering=False)
    dt_in = lambda name, shape: nc.dram_tensor(name, shape, F32, kind="ExternalInput").ap()
    x_d = dt_in("x_loc", [TW, D])
    c_d = dt_in("c_own", [128, 16])
    wa_d = dt_in("w_ada", [D, 3072])
    pp_d = dt_in("pp", [128, NP])
    win_d = dt_in("w_in", [D, 4096])
    wg_d = dt_in("w_gate", [2, 4, 256, 512])
    wp_d = dt_in("w_pool", [4, 256, 256])
    wo_d = dt_in("w_out", [D, D])
    id_d = dt_in("ident", [128, 128])
    fg_d = dt_in("fg", [D])
    bo_d = dt_in("bo", [D])
    out_d = nc.dram_tensor("out_loc", [T, D], F32, kind="ExternalOutput").ap()
    wo_bf = nc.dram_tensor("wo_bf", [D, D], BF16).ap()
    wg_bf = nc.dram_tensor("wg_bf", [2, 4, 256, 512], BF16).ap()
    cin1 = [nc.dram_tensor("cin1a", [128, 16], F32).ap(), nc.dram_tensor("cin1b", [128, 8], F32).ap()]
    cout1 = [nc.dram_tensor("cout1a", [2 * 128, 16], F32).ap(), nc.dram_tensor("cout1b", [2 * 128, 8], F32).ap()]
    cin2 = [nc.dram_tensor("cin2%s" % t, [128, 8], F32).ap() for t in "ab"]
    cout2 = [nc.dram_tensor("cout2%s" % t, [2 * 128, 8], F32).ap() for t in "ab"]
    dbg_d = {}
    for name, shape, dtp in debug:
        dbg_d[name] = nc.dram_tensor("dbg_" + name, list(shape), dtp, kind="ExternalOutput").ap()

    R_UL, R_UP, R_H, R_SG, R_PM, R_SQ, R_RS, R_WP, R_C = 0, 37440, 70976, 120384, 153152, 169728, 173824, 177920, 182016
    TOTAL = 189000 + 3 * 4096 + 8192
    R_T3 = 189000
    R_RING3 = 189000 + 3 * 4096
    if shrink_test:
        R_SG = R_UP
        R_PM, R_SQ, R_RS, R_WP, R_C = [r - 32768 for r in (R_PM, R_SQ, R_RS, R_WP, R_C)]
        TOTAL -= 32768

    from contextlib import ExitStack
    with ExitStack() as es:
        big = es.enter_context(nc.sbuf_tensor("big", [128, TOTAL], U8))
        ps = es.enter_context(nc.psum_tensor("ps", [128, 4096], F32))
        def newsem(name):
            return es.enter_context(nc.semaphore(name))
        eng_sems = {e: newsem("s_" + e) for e in ("pe", "act", "dve", "pool", "sp", "cc")}
        S_ = Sched(nc, eng_sems)
        S_.limit = limit
        def dsem(name):
            d = [newsem("d_" + name), 0]
            S_.dma_sems.append(d)
            return d

        def v(off, nbytes, dtp):
            return big[:, off:off + nbytes].bitcast(dtp)
        cpos = [R_C]
        def calloc(nelem, dtp=F32, esz=4):
            off = cpos[0]
            cpos[0] += ((nelem * esz + 63) // 64) * 64
            assert cpos[0] <= TOTAL, cpos[0]
            return v(off, nelem * esz, dtp)

        pp = calloc(NPD)
        ident_f = calloc(128)
        ident_b = calloc(128, BF16, 2)
        ones_f = calloc(128)
        cT = calloc(16)
        cact = calloc(16)
        mod_loc = calloc(24)
        mod_all = calloc(48)
        sel = calloc(48)
        cactb = calloc(16, BF16, 2)
        Gs = calloc(16)
        ss = calloc(16); sq_s = calloc(16); rstd = calloc(16)
        ss2 = calloc(8); sq2 = calloc(8); rstd2 = calloc(8)
        cko = calloc(16); ckall = calloc(32); carry = calloc(8)
        diag = calloc(128)
        def P(c, n=1):
            return pp[:, c:c + n]

        ULuc = v(R_UL, 37440, F32).rearrange("p (c t) -> p c t", c=9)
        UP = v(R_UP, 33536, F32).rearrange("p (c t) -> p c t", c=8)
        xring = [v(R_UL + s * 8192, 8192, F32) for s in range(3)]
        xn = [v(R_UL + 24576 + g * 16384, 16384, BF16).rearrange("p (i d) -> p i d", i=4) for g in range(2)]
        junk = v(R_UL + 57344, 4096, BF16)
        hT = v(R_H, 33024, BF16).rearrange("p (k t) -> p k t", k=16)
        ring = [v(R_H + 33024 + s * 8192, 8192, BF16).rearrange("p (k n) -> p k n", k=16) for s in range(2)] + \
               [v(R_RING3, 8192, BF16).rearrange("p (k n) -> p k n", k=16)]
        wa = v(R_H, 24576, BF16).rearrange("p (k n) -> p k n", k=16)
        HF = v(R_H, 32768, F32).rearrange("p (c t) -> p c t", c=8)
        Tb = [v(R_UP + i * 4096, 4096, F32) for i in range(6)] + [v(R_T3 + i * 4096, 4096, F32) for i in range(3)] + \
             [v(R_H + 32768 + i * 4096, 4096, F32) for i in range(3)]
        ucb = v(R_PM, 16384, BF16).rearrange("p (c t) -> p c t", c=8)
        woutB = v(R_UP, 32768, BF16).rearrange("p (k n) -> p k n", k=8)
        woutA = v(R_UL, 32768, BF16).rearrange("p (k n) -> p k n", k=8)
        SG = v(R_SG, 32768, BF16).rearrange("p (c t) -> p c t", c=16)
        tmpA = v(R_PM, 4192, F32); tmpB = v(R_PM + 4192, 4192, F32)
        ttmp = v(R_PM + 8384, 4096, F32)
        pooled = v(R_PM + 12480, 4096, BF16).rearrange("p (k t) -> p k t", k=2)
        wgr = [v(R_UP + 24576 + s * 2048, 2048, BF16).rearrange("p (k n) -> p k n", k=2) for s in range(3)]
        gate_bc = v(R_RING3, 8192, F32); fg_bc = v(R_PM + 8192, 8192, F32)
        SQ = v(R_SQ, 4096, F32); RS = v(R_RS, 4096, F32)
        gb_bc = v(R_SQ, 8192, F32)
        WPl = v(R_WP, 4096, BF16).rearrange("p (g k n) -> p g k n", g=4, k=2)
        xt = [v(R_H + 32768 + s * 8192, 8192, F32) for s in range(2)]
        ot = [v(R_H + s * 8192, 8192, F32) for s in range(2)]

        def bank(b, n=1):
            return ps[:, b * 512:(b + n) * 512]

        B = {}
        def bf(name):
            if name not in B:
                B[name] = Buf()
            return B[name]
        def bfs(fmt, rng):
            return [bf(fmt % i) for i in rng]
        def alias(new, olds):
            nb = bf(new)
            for o in olds:
                ob = bf(o)
                nb.r = nb.r + ob.w + ob.r

        def pe(fn, **k): return S_.op("pe", fn, **k)
        def act(fn, **k): return S_.op("act", fn, **k)
        def dve(fn, **k): return S_.op("dve", fn, **k)
        def pool(fn, **k): return S_.op("pool", fn, **k)
        def sp_dma(out, in_, sem, **k):
            return S_.op("sp", I("dma_start", out=out, in_=in_), dma_sem=sem, **k)
        def pool_dma(out, in_, sem, **k):
            return S_.op("pool", I("dma_start", out=out, in_=in_), dma_sem=sem, **k)
        def pool_cc(fn, **k):
            return S_.op("pool", fn, sig_eng="cc", **k)
        dbg_sem = dsem("dbg")
        def dump(name, src, reads):
            if name in dbg_d:
                sp_dma(dbg_d[name], src, dbg_sem, reads=reads)

        sp_dma(pp[:, 0:NP], pp_d, dsem("pp"), writes=[bf("pp")])
        sp_dma(cT, c_d, dsem("cT"), writes=[bf("cT")])
        sp_dma(ident_f, id_d, dsem("id"), writes=[bf("ident_f")])
        dve(I("memset", ap=ones_f, constant=1.0), writes=[bf("ones_f")])
        dve(I("tensor_copy", out=ident_b, in_=ident_f), reads=[bf("ident_f")], writes=[bf("ident_b")])
        dve(I("memset", ap=pp[:, C_HALF:C_HALF + 1], constant=1.0), writes=[bf("pp")])

        def tmpc(i):
            return pp[:, C_TMP + 16 * i:C_TMP + 16 * (i + 1)]
        lam = P(C_LAM, 16)
        W = [bf("pp")]
        dve(I("tensor_scalar", out=P(C_BGH, 64), in0=P(C_BG, 64), scalar1=0.5, scalar2=None, op0=ALU.mult), writes=W)
        dve(I("tensor_scalar", out=tmpc(0), in0=lam, scalar1=-1.0, scalar2=None, op0=ALU.mult), writes=W)
        dve(I("tensor_tensor", out=tmpc(0), in0=tmpc(0), in1=lam, op=ALU.max), writes=W)
        act(I("activation", out=tmpc(1), in_=tmpc(0), func=AF.Exp, scale=-1.0), writes=W)
        dve(I("tensor_scalar", out=tmpc(2), in0=tmpc(1), scalar1=2.0, scalar2=None, op0=ALU.add), writes=W)
        dve(I("reciprocal", out=tmpc(2), in_=tmpc(2)), writes=W)
        dve(I("tensor_tensor", out=tmpc(2), in0=tmpc(2), in1=tmpc(1), op=ALU.mult), writes=W)
        dve(I("tensor_tensor", out=tmpc(3), in0=tmpc(2), in1=tmpc(2), op=ALU.mult), writes=W)
        dve(I("tensor_scalar", out=tmpc(4), in0=tmpc(3), scalar1=1.0 / 13.0, scalar2=1.0 / 11.0, op0=ALU.mult, op1=ALU.add), writes=W)
        for cst in (1.0 / 9.0, 1.0 / 7.0, 1.0 / 5.0, 1.0 / 3.0, 1.0):
            dve(I("tensor_tensor", out=tmpc(4), in0=tmpc(4), in1=tmpc(3), op=ALU.mult), writes=W)
            dve(I("tensor_scalar", out=tmpc(4), in0=tmpc(4), scalar1=cst, scalar2=None, op0=ALU.add), writes=W)
        dve(I("tensor_tensor", out=tmpc(4), in0=tmpc(4), in1=tmpc(2), op=ALU.mult), writes=W)
        dve(I("tensor_scalar", out=tmpc(5), in0=lam, scalar1=-1.0, scalar2=0.0, op0=ALU.mult, op1=ALU.max), writes=W)
        dve(I("scalar_tensor_tensor", out=tmpc(5), in0=tmpc(4), scalar=2.0, in1=tmpc(5), op0=ALU.mult, op1=ALU.add), writes=W)
        dve(I("tensor_scalar", out=P(C_C1, 16), in0=tmpc(5), scalar1=-8.0, scalar2=None, op0=ALU.mult), writes=W)
        dve(I("tensor_scalar", out=P(C_C1H, 16), in0=tmpc(5), scalar1=-4.0, scalar2=None, op0=ALU.mult), writes=W)
        dve(I("tensor_scalar", out=P(C_C1Q, 16), in0=tmpc(5), scalar1=-8.0, scalar2=math.log(0.25), op0=ALU.mult, op1=ALU.add), writes=W)
        dve(I("tensor_tensor", out=P(C_S1, 8), in0=P(C_PS, 8), in1=P(C_GNP, 8), op=ALU.mult), writes=W)
        dve(I("tensor_tensor", out=P(C_S2, 8), in0=P(C_BP, 8), in1=P(C_S1, 8), op=ALU.mult), writes=W)
        dve(I("tensor_tensor", out=P(C_S3, 8), in0=P(C_BP, 8), in1=P(C_PS, 8), op=ALU.mult), writes=W)

        act(I("activation", out=cact, in_=cT, func=AF.Silu), reads=[bf("cT")], writes=[bf("cact")])
        dve(I("tensor_copy", out=cactb, in_=cact), reads=[bf("cact")], writes=[bf("cactb")])
        shift = sel[:, 0:16]; gate_fm = sel[:, 32:48]
        ada_ps = bank(7)[:, 0:48]

        win_v = win_d.rearrange("(k p) n -> p k n", p=128)
        wa_v = wa_d.rearrange("(p k) n -> p k n", k=16)
        ring_sem = [dsem("ring0"), dsem("ring1"), dsem("ring2")]
        pcount = [2]
        piece_tok = {}
        stage = [v(R_SG + s * 16384, 16384, F32).rearrange("p (k n) -> p k n", k=16) for s in range(2)]
        stage_sem = [dsem("stage0"), dsem("stage1")]
        def load_piece(kind, q):
            if kind == "ada" and q < 8:
                st = q % 2
                sp_dma(stage[st], wa_v[:, :, q * PW:(q + 1) * PW], stage_sem[st], writes=[bf("stage%d" % st)])
                s = q % 3
                slot, name = ring[s], "ring%d" % s
                tok = dve(I("tensor_copy", out=slot, in_=stage[st]), reads=[bf("stage%d" % st)], writes=[bf(name)])
                piece_tok[(kind, q)] = tok
                return (slot, name)
            s = pcount[0] % 3; pcount[0] += 1
            slot, name, sem = ring[s], "ring%d" % s, ring_sem[s]
            srcv = wa_v[:, :, q * PW:(q + 1) * PW] if kind == "ada" else win_v[:, :, PIECE_COLS[q]:PIECE_COLS[q] + PW]
            tok = pool_dma(slot, srcv, sem, writes=[bf(name)])
            piece_tok[(kind, q)] = tok
            return (slot, name)
        def ada_piece(q, sl):
            slot, sname = sl
            items = []
            for mc in range(2):
                j = q * 2 + mc
                for k in range(KC):
                    items.append(("matmul", dict(out=ada_ps[:, j:j + 1], lhsT=slot[:, k, mc * 128:(mc + 1) * 128], rhs=cactb[:, k:k + 1],
                                                 start=(k == 0), stop=(k == KC - 1))))
            pe(G(items), reads=[bf(sname), bf("cactb")], writes=[bf("bank7")])
        SEQ = [("ada", q) for q in range(8)] + [("win", q) for q in range(8)]
        for q in range(4):
            SEQ += [("ada", 8 + q), ("win", 8 + 2 * q), ("win", 9 + 2 * q)]
        slot_of = {}
        nxt = [0]
        def prefetch(upto):
            while nxt[0] < len(SEQ) and nxt[0] <= upto:
                kind, q = SEQ[nxt[0]]
                slot_of[(kind, q)] = load_piece(kind, q)
                nxt[0] += 1
        prefetch(1)
        pool_dma(WPl, wp_d.rearrange("g (k p) n -> p g k n", p=128), dsem("wp"), writes=[bf("WPl")])
        x_sem = [dsem("x0"), dsem("x1"), dsem("x2")]
        xdep = []
        NT = 9
        def rows(i):
            return (i * 128, 128) if i < 8 else (T, HALO)
        def x_load(i):
            s = i % 3
            r0, n = rows(i)
            S_.op("act", I("dma_start", out=xring[s][0:n, :], in_=x_d[r0:r0 + n, :]), dma_sem=x_sem[s], writes=[bf("xr%d" % s)])
        def xn_tile(i):
            return xn[i // 4][:, i % 4, :] if i < 8 else xn[0][0:HALO, 0, :]
        def f_square(i):
            s = i % 3; r0, n = rows(i)
            act(I("activation", out=junk[0:n, :], in_=xring[s][0:n, :], func=AF.Square, accum_out=ss[0:n, i:i + 1]),
                reads=[bf("xr%d" % s)], writes=[bf("junk"), bf("ss%d" % i)])
            act(I("activation", out=sq_s[0:n, i:i + 1], in_=ss[0:n, i:i + 1], func=AF.Ln, scale=1.0 / D, bias=EPS),
                reads=[bf("ss%d" % i)], writes=[bf("sq%d" % i)])
            act(I("activation", out=rstd[0:n, i:i + 1], in_=sq_s[0:n, i:i + 1], func=AF.Exp, scale=-0.5),
                reads=[bf("sq%d" % i)], writes=[bf("rstd%d" % i)])
        def f_scale(i):
            s = i % 3; r0, n = rows(i)
            nm = "xn%d" % i if i < 8 else "xnh"
            act(I("activation", out=xn_tile(i), in_=xring[s][0:n, :], func=AF.Copy, scale=rstd[0:n, i:i + 1]),
                reads=[bf("xr%d" % s), bf("rstd%d" % i)], writes=[bf(nm)])
        tp = [bank(5).bitcast(BF16)[:, 0:512], bank(6).bitcast(BF16)[:, 0:512]]
        tph = bank(4).bitcast(BF16)[:, 0:128].rearrange("p (j t) -> p j t", j=16)
        tpc = [0]
        def f_transposes(g):
            for j in range(KC):
                b_ = tpc[0] % 2; tpc[0] += 1
                items = [("transpose", dict(out=tp[b_][:, ii * 128:(ii + 1) * 128], in_=xn[g][:, ii, j * 128:(j + 1) * 128], identity=ident_b))
                         for ii in range(4)]
                pe(G(items), reads=bfs("xn%d", range(4 * g, 4 * g + 4)) + [bf("ident_b")], writes=[bf("tp%d" % b_)])
                dve(I("tensor_scalar", out=hT[:, j, g * 512:(g + 1) * 512], in0=tp[b_], scalar1=Gs[:, j:j + 1], scalar2=shift[:, j:j + 1],
                      op0=ALU.mult, op1=ALU.add),
                    reads=[bf("tp%d" % b_), bf("Gs"), bf("sel")], writes=[bf("hT")])
        for i in range(8):
            ada_piece(i, slot_of[("ada", i)])
            if i + 2 < 8:
                prefetch(i + 2)
        for i in range(3):
            x_load(i)
        for i in range(NT):
            f_square(i)
            if i >= 1:
                f_scale(i - 1)
                if i + 2 < NT:
                    x_load(i + 2)
        def ada_gather(t, lo, n):
            nm = "ab"[t]
            dve(I("tensor_tensor", out=mod_loc[:, lo:lo + n], in0=ada_ps[:, lo:lo + n], in1=P(C_BADA + lo, n), op=ALU.add),
                reads=[bf("bank7"), bf("pp")], writes=[bf("mod_loc" + nm)])
            bdma = pool_dma if t == 0 else sp_dma
            bdma(cin1[t], mod_loc[:, lo:lo + n], dsem("cin1" + nm), reads=[bf("mod_loc" + nm)], writes=[bf("cin1" + nm)])
            if fake_cc:
                pool_dma(cout1[t][0:128, :], cin1[t], dsem("fcc1a" + nm), reads=[bf("cin1" + nm)], writes=[bf("cout1" + nm)])
                pool_dma(cout1[t][128:256, :], cin1[t], dsem("fcc1b" + nm), reads=[bf("cin1" + nm)], writes=[bf("cout1" + nm)])
            else:
                pool_cc(I("collective_compute", kind="AllGather", op=ALU.bypass, replica_groups=[[0, 1], [2, 3], [4, 5], [6, 7]],
                          ins=[cin1[t].opt()], outs=[cout1[t].opt()]), reads=[bf("cin1" + nm)], writes=[bf("cout1" + nm)])
            mv = mod_all[:, 2 * lo:2 * lo + 2 * n].rearrange("p (r f) -> p r f", r=2)
            bdma(mv, cout1[t].rearrange("(r p) f -> p r f", p=128), dsem("mall" + nm), reads=[bf("cout1" + nm)], writes=[bf("mod_all" + nm)])
            return mv
        mva = ada_gather(0, 0, 16)
        prefetch(10)
        dve(I("tensor_copy", out=sel[:, 0:32].rearrange("p (a r i) -> p a r i", a=2, r=2), in_=mva.rearrange("p r (a i) -> p a r i", a=2)),
            reads=[bf("mod_alla")], writes=[bf("sel")])
        dve(I("scalar_tensor_tensor", out=Gs, in0=sel[:, 16:32], scalar=1.0, in1=P(C_NG, 16), op0=ALU.add, op1=ALU.mult),
            reads=[bf("sel"), bf("pp")], writes=[bf("Gs")])
        f_transposes(0)
        f_transposes(1)
        alias("xnh", ["xn0"])
        f_scale(8)
        items = [("transpose", dict(out=tph[:, j, :], in_=xn[0][0:HALO, 0, j * 128:(j + 1) * 128], identity=ident_b[0:HALO, 0:HALO]))
                 for j in range(KC)]
        pe(G(items), reads=[bf("xnh"), bf("ident_b")], writes=[bf("bank4")])
        for j in range(KC):
            dve(I("tensor_scalar", out=hT[:, j, T:TW], in0=tph[:, j, :], scalar1=Gs[:, j:j + 1], scalar2=shift[:, j:j + 1], op0=ALU.mult, op1=ALU.add),
                reads=[bf("bank4"), bf("Gs"), bf("sel")], writes=[bf("hT")])
        dump("hT", hT, [bf("hT")])

        front_bufs = ["xr0", "xr1", "xr2", "junk", "xnh"] + ["xn%d" % i for i in range(8)]
        for c in range(9):
            alias("UL%d" % c, front_bufs)
        for c in range(8):
            alias("UP%d" % c, front_bufs)
        for c in range(1, 9):
            dve(I("memset", ap=ULuc[:, c, 0:2], constant=0.0), writes=[bf("UL%d" % c)])
        for c in range(8):
            dve(I("memset", ap=UP[:, c, 0:16], constant=0.0), writes=[bf("UP%d" % c)])
        dve(I("memset", ap=tmpA[:, 0:16], constant=0.0), writes=[bf("tmpA")])
        dve(I("memset", ap=tmpB[:, 0:16], constant=0.0), writes=[bf("tmpB")])

        S_.marks.append(("inproj", S_.nops))
        for c in range(16):
            alias("SG%d" % c, ["stage0", "stage1"])
        psbuf = [bank(0, 2), bank(2, 2)]
        pshq = bank(4)[:, 0:32].rearrange("p (m t) -> p m t", m=4)
        alias("pshq", ["bank4"])
        mcount = [0]
        def inproj_piece(q):
            rslot, rname = slot_of[("win", q)]
            is_u = q < 8
            for mc in range(2):
                zc = PIECE_COLS[q] // 128 + mc
                b_ = mcount[0] % 2; mcount[0] += 1
                items = []
                for k in range(KC):
                    lhsT = rslot[:, k, mc * 128:(mc + 1) * 128]
                    items.append(("matmul", dict(out=psbuf[b_][:, 0:512], lhsT=lhsT, rhs=hT[:, k, 0:512], start=(k == 0), stop=(k == KC - 1))))
                    items.append(("matmul", dict(out=psbuf[b_][:, 512:1024], lhsT=lhsT, rhs=hT[:, k, 512:1024], start=(k == 0), stop=(k == KC - 1))))
                pe(G(items), reads=[bf(rname), bf("hT")], writes=[bf("psb%d" % b_)])
                bias = P(C_BIN + zc)
                if zc < 8:
                    act(I("activation", out=UP[:, zc, 16:16 + T], in_=psbuf[b_], func=AF.Identity, bias=bias),
                        reads=[bf("psb%d" % b_), bf("pp")], writes=[bf("UP%d" % zc)])
                elif zc < 16:
                    c = zc - 8
                    act(I("activation", out=ULuc[:, c + 1, 2:2 + T], in_=psbuf[b_], func=AF.Identity, bias=bias),
                        reads=[bf("psb%d" % b_), bf("pp")], writes=[bf("UL%d" % (c + 1))])
                else:
                    c = zc - 16
                    act(I("activation", out=SG[:, c, :], in_=psbuf[b_], func=AF.Silu, bias=bias),
                        reads=[bf("psb%d" % b_), bf("pp")], writes=[bf("SG%d" % c)])
            if is_u:
                items = []
                for mc in range(2):
                    for k in range(KC):
                        items.append(("matmul", dict(out=pshq[:, mc, :], lhsT=rslot[:, k, mc * 128:(mc + 1) * 128], rhs=hT[:, k, T:TW],
                                                     start=(k == 0), stop=(k == KC - 1))))
                pe(G(items), reads=[bf(rname), bf("hT")], writes=[bf("pshq")])
                for mc in range(2):
                    zc = PIECE_COLS[q] // 128 + mc
                    bias = P(C_BIN + zc)
                    if zc < 8:
                        act(I("activation", out=UP[:, zc, 16 + T:16 + TW], in_=pshq[:, mc, :], func=AF.Identity, bias=bias),
                            reads=[bf("pshq"), bf("pp")], writes=[bf("UP%d" % zc)])
                    else:
                        c = zc - 8
                        act(I("activation", out=ULuc[:, c + 1, 2 + T:2 + TW], in_=pshq[:, mc, :], func=AF.Identity, bias=bias),
                            reads=[bf("pshq"), bf("pp")], writes=[bf("UL%d" % (c + 1))])

        def pool_mixer(g):
            w = WINS[g]; hi = w // 2 - 1
            pooled_g = pooled2[g % 2]; pname = "pooled%d" % (g % 2)
            bufs = [(tmpA, "tmpA"), (tmpB, "tmpB")]
            for kk in range(2):
                c = 2 * g + kk
                U = UP[:, c, :]
                sh = 1; lvl = 0
                cur = curname = None
                while sh < w:
                    dst, dname = bufs[lvl % 2]
                    if lvl == 0:
                        dve(I("tensor_tensor", out=dst[:, 16:16 + TW], in0=U[:, 16:16 + TW], in1=U[:, 15:15 + TW], op=ALU.add),
                            reads=[bf("UP%d" % c)], writes=[bf(dname)])
                    else:
                        srcb, sname = bufs[(lvl - 1) % 2]
                        dve(I("tensor_tensor", out=dst[:, 16:16 + TW], in0=srcb[:, 16:16 + TW], in1=srcb[:, 16 - sh:16 - sh + TW], op=ALU.add),
                            reads=[bf(sname)], writes=[bf(dname)])
                    cur, curname = dst, dname
                    sh *= 2; lvl += 1
                Dm = cur
                dve(I("scalar_tensor_tensor", out=ttmp, in0=Dm[:, 16 + hi:16 + hi + T], scalar=P(C_AW + g), in1=U[:, 16:16 + T],
                      op0=ALU.mult, op1=ALU.subtract),
                    reads=[bf(curname), bf("UP%d" % c), bf("pp")], writes=[bf("ttmp")])
                dve(I("scalar_tensor_tensor", out=pooled_g[:, kk, :], in0=Dm[:, 17 + hi:17 + hi + T], scalar=P(C_BW + g), in1=ttmp,
                      op0=ALU.mult, op1=ALU.add),
                    reads=[bf(curname), bf("ttmp")], writes=[bf(pname)])
                t8 = ttmp[:, 0:8]
                dve(I("tensor_scalar", out=t8, in0=Dm[:, 16 + hi:24 + hi], scalar1=P(C_AL), scalar2=None, op0=ALU.mult),
                    reads=[bf(curname)], writes=[bf("ttmp")])
                dve(I("scalar_tensor_tensor", out=t8, in0=Dm[:, 17 + hi:25 + hi], scalar=P(C_BE), in1=t8, op0=ALU.mult, op1=ALU.add),
                    reads=[bf(curname)], writes=[bf("ttmp")])
                dve(I("tensor_tensor", out=t8, in0=t8, in1=P(C_INVC + 8 * g, 8), op=ALU.mult), writes=[bf("ttmp")])
                dve(I("tensor_tensor", out=pooled_g[:, kk, 0:8], in0=t8, in1=U[:, 16:24], op=ALU.subtract),
                    reads=[bf("UP%d" % c), bf("ttmp")], writes=[bf(pname)])
            dump("pooled%d" % g, pooled_g, [bf(pname)])

        ppb = [bank(5), bank(6)]
        stP = [bank(6), bank(7)]
        ACC = Tb[6]
        pooled2 = [pooled, Tb[7].bitcast(BF16).rearrange("p (k t) -> p k t", k=2)]
        ppc = [0]
        def pool_matmul(g):
            for qc in range(2):
                c = 2 * g + qc
                for n in range(2):
                    pb = ppc[0] % 2; ppc[0] += 1
                    items = [("matmul", dict(out=ppb[pb], lhsT=WPl[:, g, kk, qc * 128:(qc + 1) * 128], rhs=pooled2[g % 2][:, kk, n * 512:(n + 1) * 512],
                                             start=(kk == 0), stop=(kk == 1))) for kk in range(2)]
                    pe(G(items), reads=[bf("pooled%d" % (g % 2)), bf("WPl")], writes=[bf("ppb%d" % pb)])
                    act(I("activation", out=SQ[:, n * 512:(n + 1) * 512], in_=ppb[pb], func=AF.Square, scale=P(C_PS + c), bias=P(C_S3 + c)),
                        reads=[bf("ppb%d" % pb), bf("pp")], writes=[bf("SQ%d" % n)])
                    act(I("activation", out=UP[:, c, 16 + n * 512:16 + (n + 1) * 512], in_=ppb[pb], func=AF.Identity, scale=P(C_S1 + c), bias=P(C_S2 + c)),
                        reads=[bf("ppb%d" % pb)], writes=[bf("UP%d" % c)])
                    if c == 0:
                        pool(I("tensor_copy", out=ACC[:, n * 512:(n + 1) * 512], in_=SQ[:, n * 512:(n + 1) * 512]), reads=[bf("SQ%d" % n)],
                             writes=[bf("ACC%d" % n)])
                    else:
                        pool(I("tensor_tensor", out=ACC[:, n * 512:(n + 1) * 512], in0=ACC[:, n * 512:(n + 1) * 512], in1=SQ[:, n * 512:(n + 1) * 512],
                               op=ALU.add), reads=[bf("SQ%d" % n)], writes=[bf("ACC%d" % n)])
        def pool_stats():
            alias("stP0", ["ppb1"]); alias("stP1", ["bank7"])
            for n in range(2):
                pe(I("matmul", out=stP[n], lhsT=ones_f, rhs=ACC[:, n * 512:(n + 1) * 512], start=True, stop=True),
                   reads=[bf("ACC%d" % n), bf("ones_f")], writes=[bf("stP%d" % n)])

        def rstd_bc_from(stbanks, stfmt, nch):
            for n in range(2):
                act(I("activation", out=RS[:, n * 512:(n + 1) * 512], in_=stbanks[n], func=AF.Sqrt, scale=1.0 / nch, bias=EPS),
                    reads=[bf(stfmt % n)], writes=[bf("RS")])
            dve(I("reciprocal", out=RS, in_=RS), writes=[bf("RS")])

        def conv_chunk(c):
            src = ULuc[:, c + 1, :]; dst = ULuc[:, c, 0:T]
            dve(I("tensor_scalar", out=dst, in0=src[:, 0:T], scalar1=P(C_CW + c), scalar2=P(C_CB + c), op0=ALU.mult, op1=ALU.add),
                reads=[bf("UL%d" % (c + 1)), bf("pp")], writes=[bf("UL%d" % c)])
            for j in range(1, 5):
                dve(I("scalar_tensor_tensor", out=dst, in0=src[:, j:j + T], scalar=P(C_CW + 8 * j + c), in1=dst, op0=ALU.mult, op1=ALU.add),
                    reads=[bf("UL%d" % (c + 1))], writes=[bf("UL%d" % c)])
            dve(I("tensor_copy", out=ucb[:, c, :], in_=dst), reads=[bf("UL%d" % c)], writes=[bf("ucb%d" % c)])

        def pool_gate(c):
            dve(I("tensor_tensor", out=UP[:, c, 16:16 + T], in0=UP[:, c, 16:16 + T], in1=SG[:, c, :], op=ALU.mult),
                reads=[bf("SG%d" % c)], writes=[bf("UP%d" % c)])
            dve(I("tensor_tensor", out=SG[:, c, :], in0=UP[:, c, 16:16 + T], in1=RS, op=ALU.mult),
                reads=[bf("UP%d" % c), bf("RS")], writes=[bf("SG%d" % c)])

        seq_pos = [8]
        def run_seq_until_win(q):
            while True:
                kind, qq = SEQ[seq_pos[0]]
                if kind == "ada":
                    if qq == 8:
                        alias("bank7", ["stP1"])
                    ada_piece(qq, slot_of[(kind, qq)])
                    if qq == 11:
                        mvb = ada_gather(1, 16, 8)
                        dve(I("tensor_copy", out=sel[:, 32:48].rearrange("p (r i) -> p r i", r=2), in_=mvb), reads=[bf("mod_allb")], writes=[bf("selg")])
                else:
                    inproj_piece(qq)
                prefetch(seq_pos[0] + 3)
                seq_pos[0] += 1
                if kind == "win" and qq == q:
                    break
        wob_sem = dsem("wobf")
        wgb_sem = dsem("wgbf")
        wgb_dst = wg_bf.rearrange("d h c e -> (d h c) e"); wgb_src = wg_d.rearrange("d h c e -> (d h c) e")
        for q in range(8):
            run_seq_until_win(2 * q + 1)
            pool_dma(wo_bf[q * 256:(q + 1) * 256, :], wo_d[q * 256:(q + 1) * 256, :], wob_sem, writes=[bf("wo_bf")])
            pool_dma(wgb_dst[q * 256:(q + 1) * 256, :], wgb_src[q * 256:(q + 1) * 256, :], wgb_sem, writes=[bf("wg_bf")])
            if q == 0:
                pool_mixer(0)
            if q == 1:
                alias("ppb0", ["tp0"]); alias("ppb1", ["tp1"])
                pool_matmul(0)
                pool_mixer(1); pool_mixer(2)
            if q == 2:
                pool_matmul(1); pool_matmul(2)
                pool_mixer(3)
            if q == 3:
                pool_matmul(3)
                pool_stats()
                rstd_bc_from(stP, "stP%d", 1024)
                for c in range(8):
                    alias("ucb%d" % c, ["tmpA", "tmpB", "ttmp", "pooled0", "pooled1"])
                for c in range(0, 8):
                    conv_chunk(c)
            if q == 5:
                for c in range(8):
                    pool_gate(c)
        dump("UP", UP, bfs("UP%d", range(8)))
        dump("uc", ULuc, bfs("UL%d", range(9)))

        gps = bank(6)
        alias("gps", ["stP0", "ppb1", "tp1"])
        alias("gate_bc", ["ring2"])
        for j in range(KC):
            dve(I("tensor_scalar", out=diag, in0=ident_f, scalar1=gate_fm[:, j:j + 1], scalar2=None, op0=ALU.mult),
                reads=[bf("ident_f"), bf("selg")], writes=[bf("diag")])
            pe(I("matmul", out=gps[:, 0:128], lhsT=ones_f, rhs=diag, start=True, stop=True), reads=[bf("diag"), bf("ones_f")],
               writes=[bf("gps")])
            dve(I("tensor_copy", out=gate_bc[:, j * 128:(j + 1) * 128], in_=gps[:, 0:128]), reads=[bf("gps")], writes=[bf("gate_bc")])
        S_.marks.append(("lru", S_.nops))
        lru_old = ["hT", "ring0", "ring1"]
        up_old = ["UP%d" % c for c in range(8)]
        for i in range(6):
            alias("T%d" % i, up_old)
        for i in range(9, 12):
            alias("T%d" % i, lru_old)
        for c in range(8):
            alias("HF%d" % c, lru_old)
        for s in range(3):
            alias("wgr%d" % s, up_old)
        wg_sem = [dsem("wg0"), dsem("wg1"), dsem("wg2")]
        wg_v = wg_bf.rearrange("d h (k p) n -> p d h k n", p=128)
        def load_wg(u):
            d_, h = u // 4, u % 4
            s = u % 3
            sp_dma(wgr[s], wg_v[:, d_, h, :, :], wg_sem[s], reads=[bf("wg_bf")], writes=[bf("wgr%d" % s)])
        for u in range(3):
            load_wg(u)
        UA = [bank(0, 4), bank(4, 4)]
        alias("UB0", ["psb0", "psb1"]); alias("UB1", ["pshq", "stP0", "stP1", "ppb0", "ppb1", "tp0", "tp1", "bank7", "bank4", "gps"])
        alias("SQ", ["SQ0", "SQ1"]); alias("T6", ["ACC0", "ACC1"]); alias("T7", ["pooled1"])
        ones_bc = P(C_HALF).to_broadcast([128, T])
        units = [(d_, c) for d_ in range(2) for c in range(8)]
        def unit_bufs(ui):
            ub = ui % 2
            ts = ui % 4
            return ub, (Tb[3 * ts], Tb[3 * ts + 1], Tb[3 * ts + 2]), ("T%d" % (3 * ts), "T%d" % (3 * ts + 1), "T%d" % (3 * ts + 2))
        def lru_pe(ui):
            d_, c = units[ui]
            h, cl = c // 2, c % 2
            u = d_ * 4 + h
            ws = u % 3
            ub, _, _ = unit_bufs(ui)
            pr = UA[ub][:, 0:1024]; pi = UA[ub][:, 1024:2048]
            items = []
            for (dst, ec) in ((pr, cl), (pi, 2 + cl)):
                for n in range(2):
                    for kk in range(2):
                        items.append(("matmul", dict(out=dst[:, n * 512:(n + 1) * 512], lhsT=wgr[ws][:, kk, ec * 128:(ec + 1) * 128],
                                                     rhs=ucb[:, 2 * h + kk, n * 512:(n + 1) * 512], start=(kk == 0), stop=(kk == 1))))
            pe(G(items), reads=[bf("wgr%d" % ws), bf("ucb%d" % (2 * h)), bf("ucb%d" % (2 * h + 1))], writes=[bf("UB%d" % ub)])
            if cl == 1 and u + 3 < 8:
                load_wg(u + 3)
        def lru_actA(ui):
            d_, c = units[ui]
            h, cl = c // 2, c % 2
            ub, (T1, T2, T3), (n1, n2, n3) = unit_bufs(ui)
            pr = UA[ub][:, 0:1024]; pi = UA[ub][:, 1024:2048]
            bgr = P(C_BGH + d_ * 16 + h * 4 + cl); bgi = P(C_BGH + d_ * 16 + h * 4 + 2 + cl)
            lc = d_ * 8 + c
            act(I("activation", out=T1, in_=pr, func=AF.Tanh, scale=0.5, bias=bgr), reads=[bf("UB%d" % ub), bf("pp")], writes=[bf(n1)])
            act(I("activation", out=T2, in_=pi, func=AF.Tanh, scale=0.5, bias=bgi), reads=[bf("UB%d" % ub)], writes=[bf(n2)])
            act(I("activation", out=T3, in_=T1, func=AF.Exp, scale=P(C_C1H + lc), bias=P(C_C1H + lc)), reads=[bf(n1)], writes=[bf(n3)])
            if d_ == 0:
                dve(I("tensor_tensor", out=T1, in0=T3, in1=T3, op=ALU.mult), reads=[bf(n3)], writes=[bf(n1)])
            else:
                act(I("activation", out=T1, in_=T3, func=AF.Square), reads=[bf(n3)], writes=[bf(n1)])
            act(I("activation", out=T2, in_=T2, func=AF.Identity, bias=1.0), writes=[bf(n2)])
            pool(I("tensor_tensor", out=T2, in0=T2, in1=ULuc[:, c, 0:T], op=ALU.mult), reads=[bf("UL%d" % c)], writes=[bf(n2)])
        def lru_actS(ui):
            d_, c = units[ui]
            ub, (T1, T2, T3), (n1, n2, n3) = unit_bufs(ui)
            act(I("activation", out=T1, in_=T1, func=AF.Sqrt, scale=-0.25, bias=0.25), writes=[bf(n1)])
            dve(I("tensor_tensor", out=T2, in0=T2, in1=T1, op=ALU.mult), reads=[bf(n1)], writes=[bf(n2)])
            if d_ == 0:
                dve(I("tensor_tensor_scan", out=HF[:, c, :], data0=T3, data1=T2, initial=0.0, op0=ALU.mult, op1=ALU.add),
                    reads=[bf(n3), bf(n2)], writes=[bf("HF%d" % c)])
            else:
                dve(I("tensor_tensor_scan", out=T1[:, ::-1], data0=T3[:, ::-1], data1=T2[:, ::-1], initial=carry[:, c:c + 1],
                      op0=ALU.mult, op1=ALU.add),
                    reads=[bf(n3), bf(n2), bf("carry%d" % (c // 4))], writes=[bf(n1)])
                dve(I("tensor_tensor", out=HF[:, c, :], in0=HF[:, c, :], in1=T1, op=ALU.add), reads=[bf(n1)], writes=[bf("HF%d" % c)])
                act(I("activation", out=SQ, in_=HF[:, c, :], func=AF.Square), reads=[bf("HF%d" % c)], writes=[bf("SQ")])
                if c == 0:
                    dve(I("tensor_copy", out=RS, in_=SQ), reads=[bf("SQ")], writes=[bf("RS")])
                else:
                    pool(I("tensor_tensor", out=RS, in0=RS, in1=SQ, op=ALU.add), reads=[bf("SQ")], writes=[bf("RS")])
                dve(I("scalar_tensor_tensor", out=HF[:, c, :], in0=HF[:, c, :], scalar=P(C_GNL + c), in1=SG[:, 8 + c, :], op0=ALU.mult, op1=ALU.mult),
                    reads=[bf("SG%d" % (8 + c)), bf("pp")], writes=[bf("HF%d" % c)])
        def lru_pairs(d_, pairs):
            base = d_ * 8
            us = [base + 2 * pair + k for pair in pairs for k in range(2)]
            lru_pe(us[0]); lru_pe(us[1])
            for i, u in enumerate(us):
                lru_actA(u)
                if i + 2 < len(us):
                    lru_pe(us[i + 2])
            for u in us:
                lru_actS(u)
        def carry_exchange(t):
            nm = "ab"[t]
            dve(I("tensor_copy", out=cko[:, 8 * t:8 * t + 8], in_=HF[:, :, T - 1]), reads=bfs("HF%d", range(4 * t, 4 * t + 4)), writes=[bf("cko" + nm)])
            sp_dma(cin2[t], cko[:, 8 * t:8 * t + 8], dsem("cin2" + nm), reads=[bf("cko" + nm)], writes=[bf("cin2" + nm)])
            if fake_cc:
                pool_dma(cout2[t][0:128, :], cin2[t], dsem("fcc2" + nm), reads=[bf("cin2" + nm)], writes=[bf("cout2" + nm)])
            else:
                pool_cc(I("collective_compute", kind="AllGather", op=ALU.bypass, replica_groups=[[0, 1], [2, 3], [4, 5], [6, 7]],
                          ins=[cin2[t].opt()], outs=[cout2[t].opt()]), reads=[bf("cin2" + nm)], writes=[bf("cout2" + nm)])
            ckv = ckall[:, 16 * t:16 * t + 16].rearrange("p (r f) -> p r f", r=2)
            sp_dma(ckv, cout2[t].rearrange("(r p) f -> p r f", p=128), dsem("ckall" + nm), reads=[bf("cout2" + nm)], writes=[bf("ckall" + nm)])
            cs = slice(4 * t, 4 * t + 4)
            dve(I("tensor_scalar", out=carry[:, cs], in0=ckv[:, 0, cs], scalar1=P(C_OHP), scalar2=None, op0=ALU.mult),
                reads=[bf("ckall" + nm), bf("pp")], writes=[bf("carry%d" % t)])
            dve(I("scalar_tensor_tensor", out=carry[:, cs], in0=ckv[:, 1, cs], scalar=P(C_OHP + 1), in1=carry[:, cs], op0=ALU.mult, op1=ALU.add),
                reads=[bf("ckall" + nm)], writes=[bf("carry%d" % t)])

        S_.marks.append(("cc2", S_.nops))
        lru_pairs(0, [0, 1])
        carry_exchange(0)
        lru_pairs(0, [2, 3])
        carry_exchange(1)
        dump("hf", HF, bfs("HF%d", range(8)))
        lru_pairs(1, [0, 1])
        lru_pairs(1, [2])
        lru_pairs(1, [3])
        dump("ylru", HF, bfs("HF%d", range(8)))

        stL = [bank(0), bank(1)]
        alias("stL0", ["UB0"]); alias("stL1", ["UB0"])
        for n in range(2):
            pe(I("matmul", out=stL[n], lhsT=ones_f, rhs=RS[:, n * 512:(n + 1) * 512], start=True, stop=True),
               reads=[bf("RS"), bf("ones_f")], writes=[bf("stL%d" % n)])
        for n in range(2):
            act(I("activation", out=RS[:, n * 512:(n + 1) * 512], in_=stL[n], func=AF.Ln, scale=1.0 / 1024, bias=EPS),
                reads=[bf("stL%d" % n)], writes=[bf("RS")])
        act(I("activation", out=RS, in_=RS, func=AF.Exp, scale=-0.5), writes=[bf("RS")])
        for c in range(8):
            eng = dve if c % 3 != 2 else pool
            eng(I("tensor_tensor", out=SG[:, 8 + c, :], in0=HF[:, c, :], in1=RS, op=ALU.mult),
                reads=[bf("HF%d" % c), bf("RS")], writes=[bf("SG%d" % (8 + c))])
        dump("y", SG, bfs("SG%d", range(16)))

        S_.marks.append(("tail", S_.nops))
        alias("woutA", ["UL%d" % c for c in range(9)])
        alias("woutB", ["T%d" % i for i in range(6)] + ["wgr0", "wgr1", "wgr2"])
        wo_v = wo_bf.rearrange("(k p) n -> p k n", p=128)
        sp_dma(woutB[:, 0:4, :], wo_v[:, 8:12, :], dsem("woB0"), reads=[bf("wo_bf")], writes=[bf("woutB")])
        sp_dma(woutB[:, 4:8, :], wo_v[:, 12:16, :], dsem("woB1"), reads=[bf("wo_bf")], writes=[bf("woutB")])
        sp_dma(woutA[:, 0:4, :], wo_v[:, 0:4, :], dsem("woA0"), reads=[bf("wo_bf")], writes=[bf("woutA")])
        sp_dma(woutA[:, 4:8, :], wo_v[:, 4:8, :], dsem("woA1"), reads=[bf("wo_bf")], writes=[bf("woutA")])

        pm_old = ["tmpA", "tmpB", "ttmp", "pooled0", "pooled1"] + ["ucb%d" % c for c in range(8)]
        alias("fg_bc", pm_old)
        alias("gb_bc", ["SQ", "RS"])
        for s in range(2):
            alias("xt%d" % s, lru_old + ["T9", "T10", "T11"])
            alias("ot%d" % s, ["HF%d" % c for c in range(8)])
        sp_dma(fg_bc, fg_d.partition_broadcast(128), dsem("fg"), writes=[bf("fg_bc")])
        sp_dma(ot[0], bo_d.partition_broadcast(128), dsem("bo"), writes=[bf("ot0")])
        dve(I("tensor_tensor", out=gb_bc, in0=gate_bc, in1=ot[0], op=ALU.mult), reads=[bf("gate_bc"), bf("ot0")], writes=[bf("gb_bc")])

        xt_sem = [dsem("xt0"), dsem("xt1")]
        ot_sem = [dsem("ot0"), dsem("ot1")]
        OB = [bank(0, 4), bank(4, 4)]
        alias("OB0", ["stL0", "stL1", "UB0"]); alias("OB1", ["UB1"])
        out_toks = []
        def tail_load(i):
            s = i % 2
            sp_dma(xt[s], x_d[i * 128:(i + 1) * 128, :], xt_sem[s], writes=[bf("xt%d" % s)])
        tail_load(0); tail_load(1)
        for i in range(8):
            s = i % 2
            pool(I("tensor_tensor", out=xt[s], in0=xt[s], in1=gb_bc, op=ALU.add), reads=[bf("gb_bc")], writes=[bf("xt%d" % s)])
            items = []
            for n in range(4):
                for k in range(KC):
                    w_ = woutA if k < 8 else woutB
                    items.append(("matmul", dict(out=OB[s][:, n * 512:(n + 1) * 512], lhsT=SG[:, k, i * 128:(i + 1) * 128],
                                                 rhs=w_[:, k % 8, n * 512:(n + 1) * 512], start=(k == 0), stop=(k == KC - 1))))
            pe(G(items), reads=bfs("SG%d", range(16)) + [bf("woutA"), bf("woutB")], writes=[bf("OB%d" % s)])
            dve(I("tensor_tensor", out=ot[s], in0=OB[s], in1=gate_bc, op=ALU.mult), reads=[bf("OB%d" % s), bf("gate_bc")],
                writes=[bf("ot%d" % s)])
            dve(I("tensor_tensor", out=ot[s], in0=ot[s], in1=xt[s], op=ALU.add), reads=[bf("xt%d" % s)], writes=[bf("ot%d" % s)])
            act(I("activation", out=xt[s], in_=ot[s], func=AF.Square, accum_out=ss2[:, i:i + 1]), reads=[bf("ot%d" % s)],
                writes=[bf("xt%d" % s), bf("ss2_%d" % i)])
            act(I("activation", out=sq2[:, i:i + 1], in_=ss2[:, i:i + 1], func=AF.Sqrt, scale=1.0 / D, bias=EPS),
                reads=[bf("ss2_%d" % i)], writes=[bf("sq2_%d" % i)])
            dve(I("reciprocal", out=rstd2[:, i:i + 1], in_=sq2[:, i:i + 1]), reads=[bf("sq2_%d" % i)], writes=[bf("rstd2_%d" % i)])
            dve(I("scalar_tensor_tensor", out=ot[s], in0=ot[s], scalar=rstd2[:, i:i + 1], in1=fg_bc, op0=ALU.mult, op1=ALU.mult),
                reads=[bf("rstd2_%d" % i), bf("fg_bc")], writes=[bf("ot%d" % s)])
            out_toks.append(sp_dma(out_d[i * 128:(i + 1) * 128, :], ot[s], ot_sem[s], reads=[bf("ot%d" % s)]))
            if i + 2 < 8:
                tail_load(i + 2)
        final_toks = [t for t in out_toks[-2:] if t is not None] + ([(dbg_sem[0], dbg_sem[1])] if dbg_sem[1] else [])
        if limit is not None:
            final_toks = [(d[0], d[1]) for d in S_.dma_sems if d[1] > 0] + [(S_.sem[e], S_.cnt[e]) for e in S_.sem if S_.cnt[e] > 0]
        build_program.marks = list(S_.marks) + [("end", S_.nops)]

        with nc.Block() as block:
            @block.sync
            def _(e):
                S_.emit("sp", e)
                for (s, val) in final_toks:
                    e.wait_ge(s, val)
            @block.scalar
            def _(e):
                S_.emit("act", e)
            @block.vector
            def _(e):
                S_.emit("dve", e)
            @block.gpsimd
            def _(e):
                S_.emit("pool", e)
            @block.tensor
            def _(e):
                S_.emit("pe", e)
    return nc


def _fm(vec):
    vec = np.asarray(vec, np.float32)
    return np.ascontiguousarray(vec.reshape(-1, 128).T)


def _prep_core(c, I):
    b, half = c // 2, c % 2
    f32 = np.float32
    x = I["x"][b]
    if half == 0:
        x_loc = x[0:TW]
        dirs = (0, 1)
    else:
        x_loc = x[::-1][0:TW]
        dirs = (1, 0)
    pp = np.zeros((128, NP), f32)
    pp[:, C_NG:C_NG + 16] = _fm(I["norm_g"][0])
    pp[:, C_BIN:C_BIN + 32] = _fm(I["b_in"][0])
    cw = I["conv_w"][0]
    z = np.zeros_like(cw[0])
    taps = [cw[0], cw[1], cw[2], cw[3], z] if half == 0 else [z, cw[3], cw[2], cw[1], cw[0]]
    for j in range(5):
        pp[:, C_CW + 8 * j:C_CW + 8 * j + 8] = _fm(taps[j])
    pp[:, C_CB:C_CB + 8] = _fm(I["conv_b"][0])
    for ld in range(2):
        pp[:, C_BG + 16 * ld:C_BG + 16 * ld + 16] = _fm(I["b_gate"][0, dirs[ld]].reshape(-1))
        pp[:, C_LAM + 8 * ld:C_LAM + 8 * ld + 8] = _fm(I["lru_lambda"][0, dirs[ld]])
    pp[:, C_BP:C_BP + 8] = _fm(I["b_pool"][0].reshape(-1))
    pp[:, C_PS:C_PS + 8] = _fm(I["pool_scale"][0])
    pp[:, C_GNP:C_GNP + 8] = _fm(I["out_norm_pool_g"][0])
    pp[:, C_GNL:C_GNL + 8] = _fm(I["out_norm_lru_g"][0])
    pp[:, C_OH + b] = 1.0
    al, be = (1.0, 0.0) if half == 0 else (0.0, 1.0)
    for g, w in enumerate(WINS):
        pp[:, C_AW + g] = al / w
        pp[:, C_BW + g] = be / w
        hi, lo = w // 2 - 1, w // 2
        p = np.arange(8)
        cnt = np.minimum(w, p + (hi if half == 0 else lo) + 1)
        pp[:, C_INVC + 8 * g:C_INVC + 8 * g + 8] = (1.0 / cnt).astype(f32)[None, :]
    pp[:, C_AL] = al
    pp[:, C_BE] = be
    pp[:, C_OHP + (1 - half)] = 1.0
    ada_cols = np.concatenate([part * D + half * 1024 + np.arange(1024) for part in range(3)])
    pp[:, C_BADA:C_BADA + 24] = _fm(I["b_ada"][0][ada_cols])
    m = {
        "x_loc": np.ascontiguousarray(x_loc, f32),
        "c_own": np.ascontiguousarray(I["c"][b].reshape(128, 16), f32),
        "w_ada": np.ascontiguousarray(I["w_ada"][0][:, ada_cols], f32),
        "pp": pp,
        "w_in": np.ascontiguousarray(I["w_in"][0], f32),
        "w_gate": np.ascontiguousarray(I["w_gate"][0][list(dirs)], f32),
        "w_pool": np.ascontiguousarray(I["w_pool"][0], f32),
        "w_out": np.ascontiguousarray(I["w_out"][0], f32),
        "ident": np.eye(128, dtype=f32),
        "fg": np.ascontiguousarray(I["final_norm_g"], f32),
        "bo": np.ascontiguousarray(I["b_out"][0], f32),
    }
    return m


_NC_CACHE = {}


def kernel(**inputs):
    I = {k: np.asarray(v) for k, v in inputs.items()}
    if "nc" not in _NC_CACHE:
        _NC_CACHE["nc"] = build_program()
    nc = _NC_CACHE["nc"]
    in_maps = [_prep_core(c, I) for c in range(NCORES)]
    res = run_bass_kernel_spmd(nc, in_maps, core_ids=list(range(NCORES)))
    out = np.empty((4, S, D), np.float32)
    for c in range(NCORES):
        b, half = c // 2, c % 2
        o = np.asarray(res.results[c]["out_loc"], np.float32)
        if half == 0:
            out[b, 0:T] = o
        else:
            out[b, T:S] = o[::-1]
    return out
```
